# Optimizing a Trainium2 kernel written in Bass

```python
import jax, jax.numpy as jnp
from jax import lax
import numpy as np

D_MODEL = 1024
BATCH = 8
SEQ = 2048
DEPTH = 4
DEC_BATCH = 128
DEC_SEQ = 8
PAST_LEN = 16384
PAGE_SIZE = 128

N_META = 16
GLA_HEADS = 4
GLA_DK = D_MODEL // 2 // GLA_HEADS
GLA_DV = D_MODEL // GLA_HEADS
GLA_KEY = GLA_HEADS * GLA_DK
GLA_VAL = GLA_HEADS * GLA_DV
GATE_RANK = 16
GATE_TAU = 16.0
CHUNK = 64
CONV_CH = D_MODEL
CONV_WIDTH = 31
D_FF = 2816
EPS = 1e-6
SPLITS = (GLA_KEY, GLA_KEY, GLA_VAL, GLA_VAL, GATE_RANK, 2 * CONV_CH, D_MODEL, D_MODEL)
D_IN = GLA_KEY * 2 + GLA_VAL * 2 + GATE_RANK + 2 * CONV_CH + 2 * D_MODEL

kernel_name = "gla_conformer_conv_macaron_decode_step"


def rmsnorm(x, g):
    xf = x.astype(jnp.float32)
    y = xf * lax.rsqrt(jnp.mean(xf * xf, axis=-1, keepdims=True) + EPS)
    return y.astype(x.dtype) * g


def swiglu(x, w_gate, w_up, w_down):
    return (jax.nn.silu(x @ w_gate) * (x @ w_up)) @ w_down


def gla_segment(q, k, v, loga, S0, chunk):
    B, L, H, dk = q.shape
    dv = v.shape[-1]
    n = L // chunk
    q = q.reshape(B, n, chunk, H, dk)
    k = k.reshape(B, n, chunk, H, dk)
    v = v.reshape(B, n, chunk, H, dv)
    loga = loga.reshape(B, n, chunk, H, dk)
    b = jnp.cumsum(loga, axis=2)
    b_last = b[:, :, -1:]
    q_t = q * jnp.exp(b)
    k_t = k * jnp.exp(-b)
    scores = jnp.einsum('bnchd,bnshd->bnhcs', q_t, k_t)
    mask = jnp.tril(jnp.ones((chunk, chunk), dtype=bool))
    scores = jnp.where(mask, scores, 0.0)
    o_intra = jnp.einsum('bnhcs,bnshv->bnchv', scores, v)
    decay = jnp.exp(b_last[:, :, 0])
    upd = jnp.einsum('bnshd,bnshv->bnhdv', k * jnp.exp(b_last - b), v)

    def step(S, xs):
        dec, u = xs
        return dec[..., None] * S + u, S

    S_fin, S_start = lax.scan(step, S0, (jnp.moveaxis(decay, 1, 0), jnp.moveaxis(upd, 1, 0)))
    S_start = jnp.moveaxis(S_start, 0, 1)
    o_inter = jnp.einsum('bnchd,bnhdv->bnchv', q_t, S_start)
    return (o_intra + o_inter).reshape(B, L, H, dv), S_fin


def mixer(x, S0, conv_buf, segments, w_in, w_decay_up, b_decay, gla_norm, w_dw, b_dw, conv_norm, w_pw, w_out):
    B, L, _ = x.shape
    proj = x @ w_in
    cuts = np.cumsum(SPLITS)[:-1].tolist()
    q, k, v, g, a_lr, glu, m_a, m_b = jnp.split(proj, cuts, axis=-1)
    f32 = jnp.float32
    q = q.astype(f32).reshape(B, L, GLA_HEADS, GLA_DK) * (GLA_DK ** -0.5)
    k = k.astype(f32).reshape(B, L, GLA_HEADS, GLA_DK)
    v = v.astype(f32).reshape(B, L, GLA_HEADS, GLA_DV)
    loga = (jax.nn.log_sigmoid((a_lr @ w_decay_up + b_decay).astype(f32)) / GATE_TAU).reshape(B, L, GLA_HEADS, GLA_DK)
    S = S0.astype(f32)
    outs = []
    start = 0
    for length, chunk in segments:
        sl = slice(start, start + length)
        o, S = gla_segment(q[:, sl], k[:, sl], v[:, sl], loga[:, sl], S, chunk)
        outs.append(o)
        start += length
    o = jnp.concatenate(outs, axis=1) if len(outs) > 1 else outs[0]
    o = o * lax.rsqrt(jnp.mean(o * o, axis=-1, keepdims=True) + EPS)
    o_gla = o.reshape(B, L, GLA_VAL).astype(x.dtype) * gla_norm * jax.nn.silu(g)
    a, ga = jnp.split(glu, 2, axis=-1)
    u = a * jax.nn.sigmoid(ga)
    up = jnp.concatenate([conv_buf.astype(u.dtype), u], axis=1)
    new_buf = up[:, -(CONV_WIDTH - 1):]
    c = lax.conv_general_dilated(up, w_dw[:, None, :].astype(up.dtype), window_strides=(1,), padding='VALID',
                                 dimension_numbers=('NWC', 'WIO', 'NWC'), feature_group_count=CONV_CH) + b_dw
    c = jax.nn.silu(rmsnorm(c, conv_norm))
    o_conv = c @ w_pw
    merged = jax.nn.sigmoid(m_a) * o_gla + jax.nn.sigmoid(m_b) * o_conv
    return merged @ w_out, S.astype(x.dtype), new_buf


def trunk(h, S_list, buf_list, segments, norm_ffn1, w_ffn1_gate, w_ffn1_up, w_ffn1_down, norm_mix, w_in,
          w_decay_up, b_decay, gla_norm, w_dw, b_dw, conv_norm, w_pw, w_out, norm_ffn2, w_ffn2_gate,
          w_ffn2_up, w_ffn2_down, norm_final):
    new_S, new_buf = [], []
    for l in range(DEPTH):
        h = h + 0.5 * swiglu(rmsnorm(h, norm_ffn1[l]), w_ffn1_gate[l], w_ffn1_up[l], w_ffn1_down[l])
        m, S, buf = mixer(rmsnorm(h, norm_mix[l]), S_list[l], buf_list[l], segments, w_in[l], w_decay_up[l],
                          b_decay[l], gla_norm[l], w_dw[l], b_dw[l], conv_norm[l], w_pw[l], w_out[l])
        h = h + m
        h = h + 0.5 * swiglu(rmsnorm(h, norm_ffn2[l]), w_ffn2_gate[l], w_ffn2_up[l], w_ffn2_down[l])
        new_S.append(S)
        new_buf.append(buf)
    return rmsnorm(h, norm_final), jnp.stack(new_S), jnp.stack(new_buf)


def setup_inputs(seed: int = 0) -> dict:
    key = jax.random.key(seed)
    ks = iter(jax.random.split(key, 32))
    f32 = jnp.float32

    def w(shape, fan_in):
        return jax.random.normal(next(ks), shape, f32) * (fan_in ** -0.5)

    def gain(shape):
        return jnp.ones(shape, f32) + 0.01 * jax.random.normal(next(ks), shape, f32)

    def small(shape, s):
        return s * jax.random.normal(next(ks), shape, f32)

    return {
        "x_prompt": jax.random.normal(next(ks), (BATCH, SEQ, D_MODEL), f32),
        "x_sample": jax.random.normal(next(ks), (DEC_BATCH, DEC_SEQ, D_MODEL), f32),
        "state_gla": jax.random.normal(next(ks), (DEPTH, DEC_BATCH, GLA_HEADS, GLA_DK, GLA_DV), f32),
        "cache_conv": 0.5 * jax.random.normal(next(ks), (DEPTH, DEC_BATCH, CONV_WIDTH - 1, CONV_CH), f32),
        "meta_tokens": jax.random.normal(next(ks), (N_META, D_MODEL), f32),
        "norm_ffn1": gain((DEPTH, D_MODEL)),
        "w_ffn1_gate": w((DEPTH, D_MODEL, D_FF), D_MODEL),
        "w_ffn1_up": w((DEPTH, D_MODEL, D_FF), D_MODEL),
        "w_ffn1_down": w((DEPTH, D_FF, D_MODEL), D_FF),
        "norm_mix": gain((DEPTH, D_MODEL)),
        "w_in": w((DEPTH, D_MODEL, D_IN), D_MODEL),
        "w_decay_up": w((DEPTH, GATE_RANK, GLA_KEY), GATE_RANK),
        "b_decay": small((DEPTH, GLA_KEY), 0.1),
        "gla_norm": gain((DEPTH, GLA_VAL)),
        "w_dw": w((DEPTH, CONV_WIDTH, CONV_CH), CONV_WIDTH),
        "b_dw": small((DEPTH, CONV_CH), 0.02),
        "conv_norm": gain((DEPTH, CONV_CH)),
        "w_pw": w((DEPTH, CONV_CH, D_MODEL), CONV_CH),
        "w_out": w((DEPTH, D_MODEL, D_MODEL), D_MODEL),
        "norm_ffn2": gain((DEPTH, D_MODEL)),
        "w_ffn2_gate": w((DEPTH, D_MODEL, D_FF), D_MODEL),
        "w_ffn2_up": w((DEPTH, D_MODEL, D_FF), D_MODEL),
        "w_ffn2_down": w((DEPTH, D_FF, D_MODEL), D_FF),
        "norm_final": gain((D_MODEL,)),
    }


def reference(x_prompt, x_sample, state_gla, cache_conv, meta_tokens, norm_ffn1, w_ffn1_gate, w_ffn1_up,
              w_ffn1_down, norm_mix, w_in, w_decay_up, b_decay, gla_norm, w_dw, b_dw, conv_norm, w_pw, w_out,
              norm_ffn2, w_ffn2_gate, w_ffn2_up, w_ffn2_down, norm_final):
    weights = (norm_ffn1, w_ffn1_gate, w_ffn1_up, w_ffn1_down, norm_mix, w_in, w_decay_up, b_decay, gla_norm,
               w_dw, b_dw, conv_norm, w_pw, w_out, norm_ffn2, w_ffn2_gate, w_ffn2_up, w_ffn2_down, norm_final)
    B = x_prompt.shape[0]
    dt = x_prompt.dtype
    meta = jnp.broadcast_to(meta_tokens[None].astype(dt), (B, N_META, D_MODEL))
    h_p = jnp.concatenate([meta, x_prompt], axis=1)
    S0_p = [jnp.zeros((B, GLA_HEADS, GLA_DK, GLA_DV), dt) for _ in range(DEPTH)]
    buf0_p = [jnp.zeros((B, CONV_WIDTH - 1, CONV_CH), dt) for _ in range(DEPTH)]
    seg_p = ((N_META, N_META), (x_prompt.shape[1], min(CHUNK, x_prompt.shape[1])))
    out_p, state_gla_prompt, cache_conv_prompt = trunk(h_p, S0_p, buf0_p, seg_p, *weights)
    y_prompt = out_p[:, N_META:]
    L_s = x_sample.shape[1]
    S0_s = [state_gla[l] for l in range(DEPTH)]
    buf0_s = [cache_conv[l] for l in range(DEPTH)]
    y_sample, state_gla_sample, cache_conv_sample = trunk(x_sample, S0_s, buf0_s, ((L_s, L_s),), *weights)
    return (y_prompt, y_sample, state_gla_prompt, cache_conv_prompt, state_gla_sample, cache_conv_sample)
```

```python
import contextlib
import numpy as np
import concourse.bass as bass
import concourse.mybir as mybir
from concourse.bass_utils import run_bass_kernel_spmd

F32 = mybir.dt.float32
BF16 = mybir.dt.bfloat16
AF = mybir.ActivationFunctionType
ALU = mybir.AluOpType

D = 1024; KC = 8; DFF = 2816; FC = 22; DIN = 7184; NL = 4; NH = 4; DK = 128; DV = 256
NSEQ = 16; DSEQ = 8; NMETA = 16; SEQ = 2048; CW = 31; HIST = 30
EPS = 1e-6
TBM = 448
import os as _os
NOCACHE = _os.environ.get("NOCACHE") == "1"
OQ, OK_, OV, OG, OA, OGA, OGB, OMA, OMB = 0, 512, 1024, 2048, 3072, 3088, 4112, 5136, 6160


class Stream:
    def __init__(self, name, sem):
        self.name = name; self.sem = sem; self.count = 0; self.ops = []; self.known = {}; self.trace = []


class Chan:
    def __init__(self, name, sem):
        self.name = name; self.sem = sem; self.count = 0


class RS:
    __slots__ = ("w", "r")

    def __init__(self):
        self.w = None; self.r = {}


class KB:
    def __init__(self, nc, ctx):
        self.nc = nc; self.ctx = ctx
        self.streams = {}
        for n in ("pe", "dve", "act", "pool", "sp"):
            self.streams[n] = Stream(n, ctx.enter_context(nc.semaphore("s_" + n)))
        self.res = {}
        self.chans = []

    def chan(self, name):
        c = Chan(name, self.ctx.enter_context(self.nc.semaphore("c_" + name)))
        self.chans.append(c)
        return c

    def _st(self, key):
        s = self.res.get(key)
        if s is None:
            s = self.res[key] = RS()
        return s

    def _deps(self, reads, writes):
        deps = []
        for k in reads:
            s = self._st(k)
            if s.w is not None:
                deps.append(s.w)
        for k in writes:
            s = self._st(k)
            if s.w is not None:
                deps.append(s.w)
            deps.extend(s.r.values())
        return deps

    def _emit_waits(self, st, deps, skip_self=False):
        need = {}
        for (sem, val, owner) in deps:
            if skip_self and owner == st.name:
                continue
            if st.known.get(id(sem), 0) >= val:
                continue
            if need.get(id(sem), (None, 0))[1] < val:
                need[id(sem)] = (sem, val)
        for sem, val in need.values():
            st.known[id(sem)] = val
            st.ops.append(lambda e, sem=sem, val=val: e.wait_ge(sem, val))
            st.trace.append(("w", id(sem), val))

    def _mark(self, ticket, reads, writes, who):
        for k in reads:
            self._st(k).r[who] = ticket
        for k in writes:
            s = self._st(k); s.w = ticket; s.r = {}

    def op(self, eng, fn, reads=(), writes=(), inc=True):
        st = self.streams[eng]
        deps = self._deps(reads, writes)
        self._emit_waits(st, deps, skip_self=(eng == "pe"))
        if inc:
            st.count += 1
            sem = st.sem
            st.ops.append(lambda e, fn=fn, sem=sem: fn(e).then_inc(sem, 1))
            st.trace.append(("i", id(sem), 1))
            ticket = (st.sem, st.count, st.name)
        else:
            st.ops.append(lambda e, fn=fn: fn(e))
            ticket = (st.sem, st.count + 1, st.name)
        self._mark(ticket, reads, writes, eng)
        return ticket

    def dma(self, q, chan, out, in_, reads=(), writes=(), cont=False):
        st = self.streams[q]
        deps = self._deps(reads, writes)
        if chan.count > 0 and not cont:
            deps.append((chan.sem, 16 * chan.count, "dma"))
        self._emit_waits(st, deps)
        chan.count += 1
        sem = chan.sem
        st.ops.append(lambda e, out=out, in_=in_, sem=sem: e.dma_start(out=out, in_=in_).then_inc(sem, 16))
        st.trace.append(("i", id(sem), 16))
        ticket = (chan.sem, 16 * chan.count, "dma:" + chan.name)
        self._mark(ticket, reads, writes, "dma:" + chan.name)
        return ticket

    def wait_all(self, eng):
        st = self.streams[eng]
        for s in self.streams.values():
            if s.count > 0 and s is not st:
                st.ops.append(lambda e, sem=s.sem, v=s.count: e.wait_ge(sem, v))
        for c in self.chans:
            if c.count > 0:
                st.ops.append(lambda e, sem=c.sem, v=16 * c.count: e.wait_ge(sem, v))

    def run(self, block):
        ss = self.streams

        @block.tensor
        def _(e):
            for f in ss["pe"].ops:
                f(e)

        @block.vector
        def _(e):
            for f in ss["dve"].ops:
                f(e)

        @block.scalar
        def _(e):
            for f in ss["act"].ops:
                f(e)

        @block.gpsimd
        def _(e):
            for f in ss["pool"].ops:
                f(e)

        @block.sync
        def _(e):
            for f in ss["sp"].ops:
                f(e)


def MM(out, lhsT, rhs, start, stop):
    return lambda e: e.matmul(out, lhsT=lhsT, rhs=rhs, start=start, stop=stop)


def TR(out, in_, ident):
    return lambda e: e.transpose(out, in_, ident)


def ACTF(out, in_, func, bias=None, scale=None):
    kw = {}
    if bias is not None:
        kw["bias"] = bias
    if scale is not None:
        kw["scale"] = scale
    return lambda e: e.activation(out=out, in_=in_, func=func, **kw)


def TT(out, in0, in1, op):
    return lambda e: e.tensor_tensor(out=out, in0=in0, in1=in1, op=op)


def TS(out, in0, s1, s2, op0, op1=None):
    if op1 is None:
        return lambda e: e.tensor_scalar(out=out, in0=in0, scalar1=s1, scalar2=None, op0=op0)
    return lambda e: e.tensor_scalar(out=out, in0=in0, scalar1=s1, scalar2=s2, op0=op0, op1=op1)


def STT(out, in0, scalar, in1, op0, op1):
    return lambda e: e.scalar_tensor_tensor(out=out, in0=in0, scalar=scalar, in1=in1, op0=op0, op1=op1)


def CP(out, in_):
    return lambda e: e.tensor_copy(out=out, in_=in_)


def RCP(out, in_):
    return lambda e: e.reciprocal(out=out, in_=in_)


def MS(ap, v):
    return lambda e: e.memset(ap, v)


def make_consts():
    c = {}
    c["ident"] = np.eye(128, dtype=np.float32)
    for C in (64, 8, 16):
        s = np.arange(128)[:, None]; t = np.arange(128)[None, :]
        m = ((s // C) == (t // C)) & (s <= t)
        c["M%d" % C] = m.astype(np.float32)
        c["U%d" % C] = (m.astype(np.float32) * (-1.0 / 16.0)).astype(np.float32)
        r = ((s // C) == (t // C)) & (s > t)
        c["R%d" % C] = (r.astype(np.float32) * (-1.0 / 16.0)).astype(np.float32)
    sm = (np.arange(128)[:, None] // 8 == np.arange(16)[None, :]).astype(np.float32)
    c["seqmask"] = sm
    return c


CONST_ORDER = ["ident", "M64", "M8", "M16", "U64", "U8", "U16", "R64", "R8", "R16"]


def block_defs():
    blocks = []
    blocks.append(dict(TB=400, tiles=[("s", 0, 128, 8), ("m", 128, 16, 16), ("p", 144, 128, 64), ("p", 272, 128, 64)],
                       pc0=128, NP=272, real0=0, nreal=256, realcol=144))
    r = 256
    for b in range(4):
        blocks.append(dict(TB=448, tiles=[("p", 0, 128, 64), ("p", 128, 128, 64), ("p", 256, 128, 64), ("p", 384, 64, 64)],
                           pc0=0, NP=448, real0=r, nreal=448, realcol=0))
        r += 448
    return blocks


def build_program(nb=5, nl=NL, stop=None):
    nc = bass.Bass("TRN2", target_bir_lowering=False)

    def din(name, shape):
        return nc.dram_tensor(name, list(shape), F32, kind="ExternalInput").ap()

    def dout(name, shape):
        return nc.dram_tensor(name, list(shape), F32, kind="ExternalOutput").ap()

    xp = din("xp", [SEQ, D]); xs = din("xs", [NSEQ * DSEQ, D])
    sgi = din("sgi", [NL, NSEQ, NH, DK, DV]); cci = din("cci", [NL, NSEQ, HIST, D])
    meta = din("meta", [NMETA, D])
    norm_ffn1 = din("norm_ffn1", [NL, D]); wg1 = din("wg1", [NL, D, DFF]); wu1 = din("wu1", [NL, D, DFF]); wd1 = din("wd1", [NL, DFF, D])
    norm_mix = din("norm_mix", [NL, D]); w_in = din("w_in", [NL, D, DIN]); w_dec = din("w_dec", [NL, 16, 512]); b_dec = din("b_dec", [NL, 512])
    gla_norm = din("gla_norm", [NL, D]); w_dw = din("w_dw", [NL, CW, D]); b_dw = din("b_dw", [NL, D]); conv_norm = din("conv_norm", [NL, D])
    w_pw = din("w_pw", [NL, D, D]); w_out = din("w_out", [NL, D, D])
    norm_ffn2 = din("norm_ffn2", [NL, D]); wg2 = din("wg2", [NL, D, DFF]); wu2 = din("wu2", [NL, D, DFF]); wd2 = din("wd2", [NL, DFF, D])
    norm_final = din("norm_final", [1, D])
    cmat = din("cmat", [len(CONST_ORDER), 128, 128]); seqmask_d = din("seqmask", [128, 16])

    yp = dout("yp", [SEQ, D]); ys = dout("ys", [NSEQ * DSEQ, D])
    sgp = dout("sgp", [NL, NH, DK, DV]); ccp = dout("ccp", [NL, HIST, D])
    sgs = dout("sgs", [NL, NSEQ, NH, DK, DV]); ccs = dout("ccs", [NL, NSEQ, HIST, D])

    blocks = block_defs()[:nb]

    with contextlib.ExitStack() as ctx:
        kb = KB(nc, ctx)

        def sb(name, shape, dt):
            return ctx.enter_context(nc.sbuf_tensor(name, list(shape), dt))

        h = sb("h", [128, KC, TBM], F32)
        S_f = sb("S_f", [128, NL, NH, DV], F32)
        S_b = sb("S_b", [128, 2, NH, DV], BF16)
        hist = sb("hist", [128, NL, KC, HIST], BF16)
        vecs = sb("vecs", [128, 256], F32)
        wdwT = sb("wdwT", [128, NL, 256], F32)
        cm = sb("cm", [128, len(CONST_ORDER), 128], F32)
        seqmask = sb("seqmaskt", [128, 16], F32)
        ones_b = sb("ones_b", [128, 128], BF16)
        ones_h = sb("ones_h", [128, 128], BF16)
        wdaug = sb("wdaug", [128, 512], F32)
        xn = sb("xn", [128, KC, TBM], BF16)
        vx = sb("vx", [128, 4, 1024], BF16)
        xsq = vx[:].rearrange("p a b -> p (a b)")[:, 0:KC * TBM].rearrange("p (k t) -> p k t", t=TBM)
        rstd = sb("rstd", [128, TBM], F32)
        A1 = sb("A1", [128, FC * TBM // 2], F32)
        act = A1[:].bitcast(BF16).rearrange("p (j t) -> p j t", t=TBM)
        tmpA = sb("tmpA", [128, TBM], F32)
        tmpB = sb("tmpB", [128, TBM], F32)
        alr = sb("alr", [128, TBM], F32)
        Ltok = sb("Ltok", [128, 2, 512], F32)
        qk = sb("qk", [128, 8, TBM], F32)
        sgc = sb("sgc", [128, KC, TBM], BF16)
        uext_p = sb("uext_p", [128, KC, HIST + TBM], BF16)
        uext_s = sb("uext_s", [128, KC, NSEQ, HIST + DSEQ], BF16)
        ogla = sb("ogla", [128, KC, TBM], BF16)
        expb = sb("expb", [128, NH, 128], F32)
        expnb = sb("expnb", [128, NH, 128], F32)
        expd = sb("expd", [128, NH, 128], F32)
        kkT = sb("kkT", [128, NH, 128], F32)
        qt = sb("qt", [128, NH, 128], BF16)
        kt = sb("kt", [128, NH, 128], BF16)
        kk = sb("kk", [128, NH, 128], BF16)
        scm = sb("scm", [128, NH, 128], BF16)
        kkm = sb("kkm", [128, 2, NH, 128], BF16)
        osq = sb("osq", [128, 8, 128], BF16)
        rstdh = sb("rstdh", [128, NH, 128], F32)
        otmp = sb("otmp", [128, 8, 128], F32)
        NDIAG = 16
        diag = sb("diag", [128, NDIAG, 128], BF16)
        NSA = 6; NSB = 3
        ringA = [sb("ringA%d" % i, [128, KC, 256], BF16) for i in range(NSA)]
        ringB = [sb("ringB%d" % i, [128, FC, 128], BF16) for i in range(NSB)]
        chA = [kb.chan("ra%d" % i) for i in range(NSA)]
        chB = [kb.chan("rb%d" % i) for i in range(NSB)]
        ps = [ctx.enter_context(nc.psum_tensor("ps%d" % i, [128, 512], F32)) for i in range(8)]

        ch_init = kb.chan("init")
        ch_x = [kb.chan("x0"), kb.chan("x1")]
        ch_y = [kb.chan("y0"), kb.chan("y1")]
        ch_st = [kb.chan("st%d" % i) for i in range(4)]
        ch_sto = [kb.chan("sto%d" % i) for i in range(4)]
        ch_cc = kb.chan("cc")
        ch_cc2 = [kb.chan("cca"), kb.chan("ccb")]
        ch_misc = kb.chan("misc")
        ch_wd = kb.chan("wd")
        ch_dd = kb.chan("dd")

        block = ctx.enter_context(nc.Block())

        ident = cm[:, 0, :]
        Mmask = {64: cm[:, 1, :], 8: cm[:, 2, :], 16: cm[:, 3, :]}
        Umat = {64: cm[:, 4, :], 8: cm[:, 5, :], 16: cm[:, 6, :]}
        Rmat = {64: cm[:, 7, :], 8: cm[:, 8, :], 16: cm[:, 9, :]}

        def pk(b, c0=0, c1=512):
            return [("ps", b)]

        GR = TBM * 2

        def a1keys(byte_off, nbytes):
            return [("A1", g) for g in range(byte_off // GR, (byte_off + nbytes - 1) // GR + 1)]

        SST = 5 * GR // 4

        def stg(slot):
            return A1[:, slot * SST:slot * SST + 1024]

        def stgk(slot):
            return a1keys(slot * SST * 4, 4096)

        def vxkeys_xsq(k0, k1):
            lo = (k0 * TBM) // 1024; hi = (k1 * TBM - 1) // 1024
            return [("vx", t) for t in range(lo, hi + 1)]

        kb.dma("sp", ch_init, cm[:], cmat.rearrange("c p n -> p c n"), writes=[("cm", 0)])
        kb.dma("sp", ch_init, seqmask[:], seqmask_d[:, :], writes=[("seqmask", 0)])
        kb.op("dve", MS(ones_b[:], 1.0 / 1024.0), writes=[("ones_b", 0)])
        kb.op("dve", MS(ones_h[:], 1.0 / 256.0), writes=[("ones_h", 0)])
        kb.op("dve", MS(alr[:], 0.0), writes=[("alr", 0)])
        kb.op("dve", MS(alr[32:33, :], 1.0), writes=[("alr", 0)])
        kb.op("dve", MS(wdaug[:], 0.0), writes=[("wdaug", 0)])
        kb.op("dve", MS(S_f[:].rearrange("p l h v -> p (l h v)"), 0.0), writes=[("S_f", l, hh) for l in range(NL) for hh in range(NH)])
        kb.op("dve", MS(hist[:].rearrange("p l k t -> p (l k t)"), 0.0), writes=[("hist", l) for l in range(NL)])

        s0 = stg(0)
        for i, v_ in enumerate((norm_ffn1, norm_mix, norm_ffn2, gla_norm)):
            kb.dma("sp", ch_misc, s0[32 * i:32 * (i + 1), 0:128], v_.rearrange("l (k p) -> (l k) p", p=128), writes=stgk(0), cont=(i > 0))
        kb.op("pe", TR(ps[4][:, 0:128], s0[:, 0:128], ident), reads=stgk(0) + [("cm", 0)], writes=pk(4))
        kb.op("dve", CP(vecs[:, 0:128], ps[4][:, 0:128]), reads=pk(4), writes=[("vecs", 0)])
        s1 = stg(1)
        kb.dma("sp", ch_misc, s1[0:32, 0:128], conv_norm.rearrange("l (k p) -> (l k) p", p=128), writes=stgk(1))
        kb.dma("sp", ch_misc, s1[32:64, 0:128], b_dw.rearrange("l (k p) -> (l k) p", p=128), writes=stgk(1), cont=True)
        kb.dma("sp", ch_misc, s1[64:72, 0:128], norm_final.rearrange("l (k p) -> (l k) p", p=128), writes=stgk(1), cont=True)
        kb.op("pe", TR(ps[4][:, 128:200], s1[0:72, 0:128], ident[0:72, 0:72]), reads=stgk(1) + [("cm", 0)], writes=pk(4))
        kb.op("dve", CP(vecs[:, 128:200], ps[4][:, 128:200]), reads=pk(4), writes=[("vecs", 0)])
        for l in range(NL):
            sl = stg(2 + (l % 2))
            src = w_dw[l].rearrange("j (k p) -> (j k) p", p=128)
            kb.dma("sp", ch_misc, sl[0:128, 0:128], src[0:128, :], writes=stgk(2 + (l % 2)))
            kb.dma("sp", ch_misc, sl[0:120, 128:256], src[128:248, :], writes=stgk(2 + (l % 2)), cont=True)
            kb.op("pe", TR(ps[5][:, 0:128], sl[0:128, 0:128], ident), reads=stgk(2 + (l % 2)) + [("cm", 0)], writes=pk(5))
            kb.op("pe", TR(ps[5][:, 128:248], sl[0:120, 128:256], ident[0:120, 0:120]), reads=stgk(2 + (l % 2)), writes=pk(5))
            kb.op("dve", CP(wdwT[:, l, 0:248], ps[5][:, 0:248]), reads=pk(5), writes=[("wdwT", l)])

        def vcol(base, l, k):
            c = base + l * 8 + k
            return vecs[:, c:c + 1]

        slabs = []

        def add_slab(ring, ap, nk, ncols):
            slabs.append((ring, ap, nk, ncols))
            return len(slabs) - 1

        state = dict(next_load=0, cntA=0, cntB=0, cur=0)
        slot_of = {}
        prev_occ = {}
        last_in_slot = {}

        def plan_slots():
            ca = cb = 0
            for i, (ring, ap, nk, ncols) in enumerate(slabs):
                if ring == "A":
                    s = ("A", ca % NSA); ca += 1
                else:
                    s = ("B", cb % NSB); cb += 1
                slot_of[i] = s
                prev_occ[i] = last_in_slot.get(s, -1)
                last_in_slot[s] = i

        pend_store = []

        def scr_view(idx, nk, ncols):
            return wscr[:, scr_off[idx]:scr_off[idx] + nk * ncols].rearrange("p (k n) -> p k n", n=ncols)

        def emit_store(i):
            ring, ap, nk, ncols = slabs[i]
            rname, s = slot_of[i]
            kb.dma("pool", chSA[s], scr_view(i % per_pass, nk, ncols), ringA[s][:, 0:nk, 0:ncols],
                   reads=[("ringA", s)], writes=[("scr", i % per_pass)])

        def emit_loads(cur):
            while state["next_load"] < len(slabs) and (state["next_load"] <= cur + 2 or prev_occ[state["next_load"]] < cur):
                i = state["next_load"]
                ring, ap, nk, ncols = slabs[i]
                rname, s = slot_of[i]
                if rname == "A":
                    dst = ringA[s][:, 0:nk, 0:ncols]; ch = chA[s]
                else:
                    dst = ringB[s][:, 0:nk, 0:ncols]; ch = chB[s]
                p_ = i // per_pass
                idx = i % per_pass
                if idx == 0:
                    while pend_store:
                        emit_store(pend_store.pop(0))
                cp_ = cache_pass[idx]
                for pi in [q_ for q_ in pend_store if slot_of[q_] == slot_of[i]]:
                    pend_store.remove(pi)
                    emit_store(pi)
                if cp_ is None or p_ <= cp_:
                    kb.dma("pool", ch, dst, ap.rearrange("(k p) n -> p k n", p=128), writes=[("ring" + rname, s)])
                    if cp_ is not None and p_ == cp_:
                        pend_store.append(i)
                        if len(pend_store) > 2:
                            emit_store(pend_store.pop(0))
                else:
                    kb.dma("pool", ch, dst, scr_view(idx, nk, ncols), reads=[("scr", idx)], writes=[("ring" + rname, s)])
                state["next_load"] += 1

        class SlabIter:
            def __init__(self):
                self.i = 0

            def take(self):
                i = self.i; self.i += 1
                emit_loads(i - 2)
                rname, s = slot_of[i]
                t = ringA[s] if rname == "A" else ringB[s]
                return t, ("ring" + rname, s)

        def plan_layer(l):
            def ffn(wg, wu, wd):
                for j2 in range(0, FC, 2):
                    add_slab("A", wg[l][:, j2 * 128:(j2 + 2) * 128], KC, 256)
                    add_slab("A", wu[l][:, j2 * 128:(j2 + 2) * 128], KC, 256)
                for jo in range(KC):
                    add_slab("B", wd[l][:, jo * 128:(jo + 1) * 128], FC, 128)
            ffn(wg1, wu1, wd1)
            wi = w_in[l]
            add_slab("A", wi[:, OA:OA + 16], KC, 16)
            for o in (OQ, OQ + 256, OK_, OK_ + 256):
                add_slab("A", wi[:, o:o + 256], KC, 256)
            for i in range(4):
                add_slab("A", wi[:, OV + 256 * i:OV + 256 * (i + 1)], KC, 256)
            for i in range(4):
                add_slab("A", wi[:, OG + 256 * i:OG + 256 * (i + 1)], KC, 256)
            for i in range(4):
                add_slab("A", wi[:, OGA + 256 * i:OGA + 256 * (i + 1)], KC, 256)
                add_slab("A", wi[:, OGB + 256 * i:OGB + 256 * (i + 1)], KC, 256)
            for i in range(4):
                add_slab("A", w_pw[l][:, 256 * i:256 * (i + 1)], KC, 256)
                add_slab("A", wi[:, OMA + 256 * i:OMA + 256 * (i + 1)], KC, 256)
                add_slab("A", wi[:, OMB + 256 * i:OMB + 256 * (i + 1)], KC, 256)
            for i in range(4):
                add_slab("A", w_out[l][:, 256 * i:256 * (i + 1)], KC, 256)
            ffn(wg2, wu2, wd2)

        for b in range(len(blocks)):
            for l in range(nl):
                plan_layer(l)
        plan_slots()
        per_pass = len(slabs) // len(blocks)
        scr_off = []
        cache_pass = []
        tot = 0
        na_ = 0
        for i in range(per_pass):
            scr_off.append(tot)
            if slabs[i][0] == "A" and len(blocks) > 2 and not NOCACHE:
                tot += slabs[i][2] * slabs[i][3]
                cache_pass.append(na_ % 2)
                na_ += 1
            else:
                cache_pass.append(None)
        tot = max(tot, 16)
        wscr = nc.dram_tensor("wscr", [128, tot], BF16, kind="Internal").ap()
        chSA = [kb.chan("sa%d" % i) for i in range(NSA)]
        W = SlabIter()

        mb = dict(i=0)

        def next_bank():
            b = mb["i"] % 4; mb["i"] += 1
            return b

        hk = lambda k: ("h", k)


        def rmsnorm_stats(src, src_keys, TB):
            kb.op("act", ACTF(xsq[:, :, 0:TB], src, AF.Square), reads=src_keys, writes=[("vx", t) for t in range(4)])
            for k in range(KC):
                kb.op("pe", MM(ps[4][:, 0:TB], ones_b[:], xsq[:, k, 0:TB], k == 0, k == KC - 1),
                      reads=[("vx", t) for t in range(4)] + [("ones_b", 0)], writes=pk(4), inc=(k == KC - 1))
            rsqrt_ps(rstd[:, 0:TB], ps[4][:, 0:TB], pk(4), [("rstd", 0)])

        esq = dict(pending=[], n=0)

        def esq_mm(k, TB, stop):
            kb.op("pe", MM(ps[4][:, 0:TB], ones_b[:], xsq[:, k, 0:TB], esq["n"] == 0, stop),
                  reads=vxkeys_xsq(k, k + 1) + [("ones_b", 0)], writes=pk(4), inc=stop)
            esq["n"] += 1

        def esq_chunk(k, TB, depth=1):
            kb.op("act", ACTF(xsq[:, k, 0:TB], h[:, k, 0:TB], AF.Square), reads=[hk(k)], writes=vxkeys_xsq(k, k + 1))
            esq["pending"].append(k)
            while len(esq["pending"]) > depth:
                esq_mm(esq["pending"].pop(0), TB, False)

        def rsqrt_ps(dst, src_ps, rkeys, wkeys):
            kb.op("act", ACTF(dst, src_ps, AF.Ln, bias=EPS), reads=rkeys, writes=wkeys)
            kb.op("act", ACTF(dst, dst, AF.Exp, scale=-0.5), reads=wkeys, writes=wkeys)

        def esq_finish(TB):
            while esq["pending"]:
                k = esq["pending"].pop(0)
                esq_mm(k, TB, len(esq["pending"]) == 0)
            esq["n"] = 0
            rsqrt_ps(rstd[:, 0:TB], ps[4][:, 0:TB], pk(4), [("rstd", 0)])

        def rmsnorm_to_xn(l, base, TB, pre=False):
            if pre:
                esq_finish(TB)
            else:
                rmsnorm_stats(h[:, :, 0:TB], [hk(k) for k in range(KC)], TB)
            for k in range(KC):
                kb.op("dve", STT(xn[:, k, 0:TB], h[:, k, 0:TB], vcol(base, l, k), rstd[:, 0:TB], ALU.mult, ALU.mult),
                      reads=[hk(k), ("rstd", 0), ("vecs", 0)], writes=[("xn", k)])

        def ffn(l, base, TB, pre):
            rmsnorm_to_xn(l, base, TB, pre)
            for j2 in range(0, FC, 2):
                wgt, wgk = W.take()
                wut, wuk = W.take()
                banks = [(next_bank(), next_bank()) for jj in range(2)]
                if j2 == 0:
                    for k in range(KC):
                        for jj in range(2):
                            bg, bu = banks[jj]
                            kb.op("pe", MM(ps[bg][:, 0:TB], wgt[:, k, jj * 128:(jj + 1) * 128], xn[:, k, 0:TB], k == 0, k == KC - 1),
                                  reads=[wgk, ("xn", k)], writes=pk(bg), inc=(k == KC - 1))
                            kb.op("pe", MM(ps[bu][:, 0:TB], wut[:, k, jj * 128:(jj + 1) * 128], xn[:, k, 0:TB], k == 0, k == KC - 1),
                                  reads=[wuk, ("xn", k)], writes=pk(bu), inc=(k == KC - 1))
                for jj in range(2):
                    j = j2 + jj
                    bg, bu = banks[jj]
                    if j2 != 0:
                        for k in range(KC):
                            kb.op("pe", MM(ps[bg][:, 0:TB], wgt[:, k, jj * 128:(jj + 1) * 128], xn[:, k, 0:TB], k == 0, k == KC - 1),
                                  reads=[wgk, ("xn", k)], writes=pk(bg), inc=(k == KC - 1))
                        for k in range(KC):
                            kb.op("pe", MM(ps[bu][:, 0:TB], wut[:, k, jj * 128:(jj + 1) * 128], xn[:, k, 0:TB], k == 0, k == KC - 1),
                                  reads=[wuk, ("xn", k)], writes=pk(bu), inc=(k == KC - 1))
                    tmp = tmpA if (j % 2 == 0) else tmpB
                    tk = ("tmp", j % 2)
                    kb.op("act", ACTF(tmp[:, 0:TB], ps[bg][:, 0:TB], AF.Silu), reads=pk(bg), writes=[tk])
                    kb.op("dve", TT(act[:, j, 0:TB], tmp[:, 0:TB], ps[bu][:, 0:TB], ALU.mult), reads=[tk] + pk(bu), writes=[("A1", j)])
            for jo in range(KC):
                wdt, wdk = W.take()
                bo = next_bank()
                for j in range(FC):
                    kb.op("pe", MM(ps[bo][:, 0:TB], wdt[:, j, 0:128], act[:, j, 0:TB], j == 0, j == FC - 1),
                          reads=[wdk, ("A1", j)], writes=pk(bo), inc=(j == FC - 1))
                kb.op("dve", STT(h[:, jo, 0:TB], ps[bo][:, 0:TB], 0.5, h[:, jo, 0:TB], ALU.mult, ALU.add),
                      reads=pk(bo) + [hk(jo)], writes=[hk(jo)])
                esq_chunk(jo, TB)

        def load_block_input(bi, blk):
            for ti, (kind, c0, ntok, C) in enumerate(blk["tiles"]):
                slot = ti % 2
                st_ = stg(slot)
                if kind == "s":
                    src = xs[0:128, :]
                elif kind == "m":
                    src = meta[0:16, :]
                else:
                    r0 = blk["real0"] + (c0 - blk["realcol"])
                    src = xp[r0:r0 + ntok, :]
                kb.dma("sp", ch_x[slot], st_[0:ntok, :], src, writes=stgk(slot))
                for half in range(2):
                    pb = ps[5 + half]
                    for kk_ in range(4):
                        k = half * 4 + kk_
                        kb.op("pe", TR(pb[:, kk_ * 128:kk_ * 128 + ntok], st_[0:ntok, k * 128:(k + 1) * 128], ident[0:ntok, 0:ntok]),
                              reads=stgk(slot) + [("cm", 0)], writes=pk(5 + half), inc=(kk_ == 3))
                    src_v = pb[:, :].rearrange("p (a b) -> p a b", b=128)[:, :, 0:ntok]
                    kb.op("act" if half == 0 else "dve",
                          (ACTF(h[:, half * 4:half * 4 + 4, c0:c0 + ntok], src_v, AF.Copy) if half == 0
                           else CP(h[:, half * 4:half * 4 + 4, c0:c0 + ntok], src_v)),
                          reads=pk(5 + half), writes=[hk(k) for k in range(half * 4, half * 4 + 4)])

        def store_block_output(bi, blk):
            TB = blk["TB"]
            esq_finish(TB)
            for k in range(KC):
                kb.op("dve", STT(h[:, k, 0:TB], h[:, k, 0:TB], vecs[:, 192 + k:193 + k], rstd[:, 0:TB], ALU.mult, ALU.mult),
                      reads=[hk(k), ("rstd", 0), ("vecs", 0)], writes=[hk(k)])
            for ti, (kind, c0, ntok, C) in enumerate(blk["tiles"]):
                if kind == "m":
                    continue
                slot = 2 + (ti % 2)
                st_ = stg(slot)
                for half in range(2):
                    pb = ps[5 + half]
                    for kk_ in range(4):
                        k = half * 4 + kk_
                        kb.op("pe", TR(pb[0:ntok, kk_ * 128:(kk_ + 1) * 128], h[:, k, c0:c0 + ntok], ident),
                              reads=[hk(k), ("cm", 0)], writes=pk(5 + half), inc=(kk_ == 3))
                    kb.op("act" if half == 0 else "dve",
                          (ACTF(st_[0:ntok, half * 512:(half + 1) * 512], pb[0:ntok, :], AF.Copy) if half == 0
                           else CP(st_[0:ntok, half * 512:(half + 1) * 512], pb[0:ntok, :])),
                          reads=pk(5 + half), writes=stgk(slot))
                if kind == "s":
                    dst = ys[0:128, :]
                else:
                    r0 = blk["real0"] + (c0 - blk["realcol"])
                    dst = yp[r0:r0 + ntok, :]
                kb.dma("sp", ch_y[ti % 2], dst, st_[0:ntok, :], reads=stgk(slot))

        def mixer(l, bi, blk):
            TB = blk["TB"]; tiles = blk["tiles"]; NP = blk["NP"]; pc0 = blk["pc0"]
            has_s = (bi == 0)
            last = (bi == 4)
            rmsnorm_to_xn(l, 32, TB, True)
            kb.dma("sp", ch_wd, wdaug[0:16, :], w_dec[l], writes=[("wdaug", 0)])
            kb.dma("sp", ch_wd, wdaug[32:33, :], b_dec[l:l + 1, :], writes=[("wdaug", 0)], cont=True)
            if bi == 0:
                kb.op("dve", MS(uext_p[:, :, 0:HIST], 0.0), writes=[("uext_p", k) for k in range(KC)])
            else:
                kb.op("dve", CP(uext_p[:, :, 0:HIST], hist[:, l, :, :]), reads=[("hist", l)], writes=[("uext_p", k) for k in range(KC)])
            par = dict(p=0)
            kb.op("act", ACTF(S_b[:, 0, :, :], S_f[:, l, :, :], AF.Copy), reads=[("S_f", l, hh) for hh in range(NH)],
                  writes=[("S_b", 0)])

            wt_a, wk_a = W.take()
            bnk_a = next_bank()
            wt_q, wk_q = W.take()
            bq = [next_bank(), next_bank()]
            for k in range(KC):
                kb.op("pe", MM(ps[bnk_a][0:16, 0:TB], wt_a[:, k, 0:16], xn[:, k, 0:TB], k == 0, k == KC - 1),
                      reads=[wk_a, ("xn", k)], writes=pk(bnk_a), inc=(k == KC - 1))
                for jj in range(2):
                    kb.op("pe", MM(ps[bq[jj]][:, 0:TB], wt_q[:, k, jj * 128:(jj + 1) * 128], xn[:, k, 0:TB], k == 0, k == KC - 1),
                          reads=[wk_q, ("xn", k)], writes=pk(bq[jj]), inc=(k == KC - 1))
            kb.op("dve", CP(alr[0:16, 0:TB], ps[bnk_a][0:16, 0:TB]), reads=pk(bnk_a), writes=[("alr", 0)])
            def emit_z(tj):
                _, c0z, ntz, _ = tiles[tj]
                ltz = Ltok[:, tj % 2, :]
                ltkz = ("Ltok", tj % 2)
                kb.op("pe", MM(ps[0][0:ntz, 0:512], alr[0:33, c0z:c0z + ntz], wdaug[0:33, :], True, True),
                      reads=[("alr", 0), ("wdaug", 0)], writes=pk(0))
                kb.op("act", ACTF(ltz[0:ntz, :], ps[0][0:ntz, 0:512], AF.Exp, scale=-1.0), reads=pk(0), writes=[ltkz])
                kb.op("act", ACTF(ltz[0:ntz, :], ltz[0:ntz, :], AF.Ln, bias=1.0), reads=[ltkz], writes=[ltkz])

            for i in range(4):
                if i > 0:
                    wt, wk = W.take()
                for jj in range(2):
                    c = i * 2 + jj
                    if i == 0:
                        bnk = bq[jj]
                    else:
                        bnk = next_bank()
                        for k in range(KC):
                            kb.op("pe", MM(ps[bnk][:, 0:TB], wt[:, k, jj * 128:(jj + 1) * 128], xn[:, k, 0:TB], k == 0, k == KC - 1),
                                  reads=[wk, ("xn", k)], writes=pk(bnk), inc=(k == KC - 1))
                    if c < 4:
                        kb.op("act", ACTF(qk[:, c, 0:TB], ps[bnk][:, 0:TB], AF.Copy, scale=float(DK ** -0.5)),
                              reads=pk(bnk), writes=[("qk", c)])
                    else:
                        kb.op("dve", CP(qk[:, c, 0:TB], ps[bnk][:, 0:TB]), reads=pk(bnk), writes=[("qk", c)])
            emit_z(0)
            for i in range(4):
                wt, wk = W.take()
                for ti, (kind, c0, ntok, C) in enumerate(tiles):
                    bnk = next_bank()
                    for k in range(KC):
                        kb.op("pe", MM(ps[bnk][0:ntok, 0:256], xn[:, k, c0:c0 + ntok], wt[:, k, 0:256], k == 0, k == KC - 1),
                              reads=[wk, ("xn", k)], writes=pk(bnk), inc=(k == KC - 1))
                    if ti % 2 == 0:
                        kb.op("act", ACTF(vx[0:ntok, ti, i * 256:(i + 1) * 256], ps[bnk][0:ntok, 0:256], AF.Copy),
                              reads=pk(bnk), writes=[("vx", ti)])
                    else:
                        kb.op("dve", CP(vx[0:ntok, ti, i * 256:(i + 1) * 256], ps[bnk][0:ntok, 0:256]),
                              reads=pk(bnk), writes=[("vx", ti)])
            for i in range(4):
                wt, wk = W.take()
                for jj in range(2):
                    c = i * 2 + jj
                    bnk = next_bank()
                    for k in range(KC):
                        kb.op("pe", MM(ps[bnk][:, 0:TB], wt[:, k, jj * 128:(jj + 1) * 128], xn[:, k, 0:TB], k == 0, k == KC - 1),
                              reads=[wk, ("xn", k)], writes=pk(bnk), inc=(k == KC - 1))
                    tmp = tmpA if (c % 2 == 0) else tmpB
                    tk = ("tmp", c % 2)
                    kb.op("act", ACTF(tmp[:, 0:TB], ps[bnk][:, 0:TB], AF.Silu), reads=pk(bnk), writes=[tk])
                    kb.op("dve", TS(sgc[:, c, 0:TB], tmp[:, 0:TB], vcol(96, l, c), None, ALU.mult), reads=[tk, ("vecs", 0)], writes=[("sgc", c)])
            if has_s:
                for g4 in range(4):
                    cs_ = 2 + g4 % 2
                    st_ = stg(cs_)
                    kb.dma("sp", ch_cc2[g4 % 2], st_[0:120, :], cci[l, g4 * 4:(g4 + 1) * 4].rearrange("s t d -> (s t) d"), writes=stgk(cs_))
                    for half in range(2):
                        pb = ps[5 + half]
                        for kk_ in range(4):
                            k = half * 4 + kk_
                            kb.op("pe", TR(pb[:, kk_ * 128:kk_ * 128 + 120], st_[0:120, k * 128:(k + 1) * 128], ident[0:120, 0:120]),
                                  reads=stgk(cs_) + [("cm", 0)], writes=pk(5 + half), inc=(kk_ == 3))
                        src_v = pb[:, :].rearrange("p (a b) -> p a b", b=128)[:, :, 0:120].rearrange("p a (s t) -> p a s t", t=HIST)
                        kb.op("dve", CP(uext_s[:, half * 4:half * 4 + 4, g4 * 4:(g4 + 1) * 4, 0:HIST], src_v),
                              reads=pk(5 + half), writes=[("uext_s", k) for k in range(half * 4, half * 4 + 4)])
                kb.dma("sp", ch_dd, ccs[l, :, 0:HIST - DSEQ, :], cci[l, :, DSEQ:HIST, :])
            ufp_ = stg(2)
            ufp = ufp_.rearrange("p (k t) -> p k t", t=128)
            need_ufp = has_s or last
            for i in range(4):
                wa, wak = W.take()
                wg_, wgk = W.take()
                for jj in range(2):
                    c = i * 2 + jj
                    ba = next_bank(); bg = next_bank()
                    for k in range(KC):
                        kb.op("pe", MM(ps[ba][:, 0:TB], wa[:, k, jj * 128:(jj + 1) * 128], xn[:, k, 0:TB], k == 0, k == KC - 1),
                              reads=[wak, ("xn", k)], writes=pk(ba), inc=(k == KC - 1))
                    for k in range(KC):
                        kb.op("pe", MM(ps[bg][:, 0:TB], wg_[:, k, jj * 128:(jj + 1) * 128], xn[:, k, 0:TB], k == 0, k == KC - 1),
                              reads=[wgk, ("xn", k)], writes=pk(bg), inc=(k == KC - 1))
                    tmp = tmpA if (c % 2 == 0) else tmpB
                    tk = ("tmp", c % 2)
                    kb.op("act", ACTF(tmp[:, 0:TB], ps[bg][:, 0:TB], AF.Sigmoid), reads=pk(bg), writes=[tk])
                    kb.op("dve", TT(uext_p[:, c, HIST:HIST + NP], ps[ba][:, pc0:pc0 + NP], tmp[:, pc0:pc0 + NP], ALU.mult),
                          reads=pk(ba) + [tk], writes=[("uext_p", c)])
                    if has_s:
                        kb.op("dve", TT(uext_s[:, c, :, HIST:HIST + DSEQ], ps[ba][:, 0:128].rearrange("p (s t) -> p s t", t=DSEQ),
                                        tmp[:, 0:128].rearrange("p (s t) -> p s t", t=DSEQ), ALU.mult),
                              reads=pk(ba) + [tk], writes=[("uext_s", c)])
                        kb.op("dve", TT(ufp[:, c, 0:128], ps[ba][:, 0:128], tmp[:, 0:128], ALU.mult),
                              reads=pk(ba) + [tk], writes=stgk(2))
                    if last:
                        kb.op("dve", TT(ufp[:, c, 0:HIST], ps[ba][:, TB - HIST:TB], tmp[:, TB - HIST:TB], ALU.mult),
                              reads=pk(ba) + [tk], writes=stgk(2))
            kb.op("act", ACTF(hist[:, l, :, :], uext_p[:, :, NP:NP + HIST], AF.Copy),
                  reads=[("uext_p", k) for k in range(KC)], writes=[("hist", l)])
            if need_ufp:
                n_u = 128 if has_s else HIST
                st_ = stg(3)
                for half in range(2):
                    pb = ps[5 + half]
                    for kk_ in range(4):
                        k = half * 4 + kk_
                        kb.op("pe", TR(pb[0:n_u, kk_ * 128:(kk_ + 1) * 128], ufp[:, k, 0:n_u], ident),
                              reads=stgk(2) + [("cm", 0)], writes=pk(5 + half), inc=(kk_ == 3))
                    kb.op("dve", CP(st_[0:n_u, half * 512:(half + 1) * 512], pb[0:n_u, :]), reads=pk(5 + half), writes=stgk(3))
                if has_s:
                    for s_ in range(NSEQ):
                        kb.dma("sp", ch_cc, ccs[l, s_, HIST - DSEQ:HIST, :], st_[s_ * DSEQ:(s_ + 1) * DSEQ, :], reads=stgk(3), cont=(s_ > 0))
                else:
                    kb.dma("sp", ch_cc, ccp[l, :, :], st_[0:HIST, :], reads=stgk(3))

            if has_s:
                cpre = qk
                cpk = lambda c: [("qk", c)]
            else:
                cpre = A1[:, 0:KC * TBM].rearrange("p (k t) -> p k t", t=TBM)
                cpk = lambda c: a1keys(c * TBM * 4, TBM * 4)
            dseq = [(c_, j_) for c_ in range(KC) for j_ in range(CW)]
            dst_ = dict(built=0, used=0)
            cbank = {}

            def build_diag(i):
                c_, j_ = dseq[i]
                ds = i % NDIAG
                wcol = wdwT[:, l, j_ * 8 + c_:j_ * 8 + c_ + 1]
                if ds % 2 == 0:
                    kb.op("dve", TS(diag[:, ds, :], ident, wcol, None, ALU.mult), reads=[("cm", 0), ("wdwT", l)], writes=[("diag", ds)])
                else:
                    kb.op("act", ACTF(diag[:, ds, :], ident, AF.Copy, scale=wcol), reads=[("cm", 0), ("wdwT", l)], writes=[("diag", ds)])

            def prefetch_diags(ahead=NDIAG):
                while dst_["built"] < len(dseq) and dst_["built"] < dst_["used"] + ahead:
                    build_diag(dst_["built"]); dst_["built"] += 1

            def conv_part(c, j0, j1):
                if j0 == 0:
                    cbank[c] = (next_bank(), next_bank()) if has_s else (7, None)
                bnk, bnk2 = cbank[c]
                for j in range(j0, j1):
                    i = c * CW + j
                    assert i == dst_["used"]
                    if i >= dst_["built"]:
                        build_diag(i); dst_["built"] = i + 1
                    ds = i % NDIAG
                    dk_ = ("diag", ds)
                    kb.op("pe", MM(ps[bnk][:, pc0:pc0 + NP], diag[:, ds, :], uext_p[:, c, j:j + NP], j == 0, j == CW - 1),
                          reads=[dk_, ("uext_p", c)], writes=pk(bnk), inc=True)
                    if has_s:
                        kb.op("pe", MM(ps[bnk2][:, 0:128].rearrange("p (s t) -> p s t", t=DSEQ), diag[:, ds, :], uext_s[:, c, :, j:j + DSEQ], j == 0, j == CW - 1),
                              reads=[dk_, ("uext_s", c)], writes=pk(bnk2), inc=True)
                    dst_["used"] = i + 1
                if j1 == CW:
                    kb.op("dve", TS(cpre[:, c, pc0:pc0 + NP], ps[bnk][:, pc0:pc0 + NP], vcol(160, l, c), None, ALU.add),
                          reads=pk(bnk) + [("vecs", 0)], writes=cpk(c))
                    if has_s:
                        kb.op("dve", TS(cpre[:, c, 0:128], ps[bnk2][:, 0:128], vcol(160, l, c), None, ALU.add),
                              reads=pk(bnk2) + [("vecs", 0)], writes=cpk(c))

            def conv_chunk(c):
                conv_part(c, 0, CW)
                prefetch_diags()

            if not has_s:
                prefetch_diags()

            def H4(t):
                return t.rearrange("p (h t) -> p h t", t=128)

            pend_epi = []
            qkeys = [("qk", i) for i in range(4)]
            kkeys = [("qk", 4 + i) for i in range(4)]
            for ti, (kind, c0, ntok, C) in enumerate(tiles):
                nch = ntok // C
                lt = Ltok[:, ti % 2, :]
                ltk = ("Ltok", ti % 2)
                for hh in range(NH):
                    kb.op("pe", MM(ps[1][:, hh * 128:hh * 128 + ntok], lt[0:ntok, hh * 128:(hh + 1) * 128], Umat[C][0:ntok, 0:ntok], True, True),
                          reads=[ltk, ("cm", 0)], writes=pk(1), inc=(hh == NH - 1))
                for hh in range(NH):
                    kb.op("pe", MM(ps[2][:, hh * 128:hh * 128 + ntok], lt[0:ntok, hh * 128:(hh + 1) * 128], Rmat[C][0:ntok, 0:ntok], True, True),
                          reads=[ltk, ("cm", 0)], writes=pk(2), inc=(hh == NH - 1))
                if ti + 1 < len(tiles):
                    emit_z(ti + 1)
                if not has_s:
                    conv_part(2 * ti, 0, CW)
                b1 = H4(ps[1][:, :])[:, :, 0:ntok]
                b2 = H4(ps[2][:, :])[:, :, 0:ntok]
                kb.op("act", ACTF(expb[:, :, 0:ntok], b1, AF.Exp), reads=pk(1), writes=[("g_expb", 0)])
                kb.op("act", ACTF(expnb[:, :, 0:ntok], b1, AF.Exp, scale=-1.0), reads=pk(1), writes=[("g_expnb", 0)])
                kb.op("act", ACTF(expd[:, :, 0:ntok], b2, AF.Exp), reads=pk(2), writes=[("g_expd", 0)])
                qv = qk[:, 0:4, c0:c0 + ntok]
                kv = qk[:, 4:8, c0:c0 + ntok]
                kb.op("dve", TT(qt[:, :, 0:ntok], qv, expb[:, :, 0:ntok], ALU.mult), reads=qkeys + [("g_expb", 0)], writes=[("g_qt", 0)])
                kb.op("dve", TT(kt[:, :, 0:ntok], kv, expnb[:, :, 0:ntok], ALU.mult), reads=kkeys + [("g_expnb", 0)], writes=[("g_kt", 0)])
                kb.op("dve", TT(kkT[:, :, 0:ntok], kv, expd[:, :, 0:ntok], ALU.mult), reads=kkeys + [("g_expd", 0)], writes=[("g_kkT", 0)])
                qt32 = expnb
                if kind == "s":
                    kb.op("dve", TT(qt32[:, :, 0:ntok], qv, expb[:, :, 0:ntok], ALU.mult), reads=qkeys + [("g_expb", 0)], writes=[("g_expnb", 0)])
                if not has_s:
                    prefetch_diags()
                if pend_epi:
                    pend_epi.pop(0)()
                for hh in range(NH):
                    kb.op("pe", MM(ps[3][0:ntok, hh * 128:hh * 128 + ntok], kt[:, hh, 0:ntok], qt[:, hh, 0:ntok], True, True),
                          reads=[("g_kt", 0), ("g_qt", 0)], writes=pk(3), inc=(hh == NH - 1))
                for hh in range(NH):
                    kb.op("pe", TR(ps[4][0:ntok, hh * 128:(hh + 1) * 128], kkT[:, hh, 0:ntok], ident),
                          reads=[("g_kkT", 0), ("cm", 0)], writes=pk(4), inc=(hh == NH - 1))
                if not has_s:
                    conv_part(2 * ti + 1, 0, 15)
                for hh in range(NH):
                    kb.op("dve", TT(scm[0:ntok, hh, 0:ntok], ps[3][0:ntok, hh * 128:hh * 128 + ntok], Mmask[C][0:ntok, 0:ntok], ALU.mult),
                          reads=pk(3) + [("cm", 0)], writes=[("g_scm", 0)])
                kb.op("act", ACTF(kk[0:ntok, :, :], H4(ps[4][0:ntok, :]), AF.Copy), reads=pk(4), writes=[("g_kk", 0)])
                if not has_s:
                    prefetch_diags()
                def s_load(jn):
                    kb.dma("sp", ch_st[jn % 4], stg(jn % 4).rearrange("p (h v) -> p h v", v=DV),
                           sgi[l, jn].rearrange("h d v -> d h v"), writes=stgk(jn % 4))
                if kind == "s":
                    for jn in range(3):
                        s_load(jn)
                for j in range(nch):
                    sp_ = (par["p"] + j) % 2
                    if kind == "s":
                        slot = j % 4
                        sst = stg(slot).rearrange("p (h v) -> p h v", v=DV)
                        if j + 3 < nch:
                            s_load(j + 3)
                        km = j % 2
                        if j == 0:
                            kb.op("dve", TS(kkm[:, 0, :, :], kk[:, :, :], seqmask[:, 0:1], None, ALU.mult),
                                  reads=[("g_kk", 0), ("seqmask", 0)], writes=[("kkm", 0)])
                    def emit_o():
                        for hh in range(NH):
                            ob = 5 + hh // 2
                            for e_ in range(2):
                                oc = (hh % 2) * 256 + e_ * 128
                                if kind == "s":
                                    kb.op("pe", MM(ps[ob][:, oc + j * C:oc + (j + 1) * C], sst[:, hh, e_ * 128:(e_ + 1) * 128],
                                                   qt32[:, hh, j * C:(j + 1) * C], True, False),
                                          reads=stgk(slot) + [("g_expnb", 0)], writes=pk(ob), inc=False)
                                else:
                                    kb.op("pe", MM(ps[ob][:, oc + j * C:oc + (j + 1) * C], S_b[:, sp_, hh, e_ * 128:(e_ + 1) * 128],
                                                   qt[:, hh, j * C:(j + 1) * C], True, False),
                                          reads=[("S_b", sp_), ("g_qt", 0)], writes=pk(ob), inc=False)
                                kb.op("pe", MM(ps[ob][:, oc + j * C:oc + (j + 1) * C], vx[0:ntok, ti, hh * 256 + e_ * 128:hh * 256 + (e_ + 1) * 128],
                                               scm[0:ntok, hh, j * C:(j + 1) * C], False, True),
                                      reads=[("vx", ti), ("g_scm", 0)], writes=pk(ob), inc=True)
                    def emit_upd():
                        for hh in range(NH):
                            ub = 1 + hh // 2; uc = (hh % 2) * 256
                            if kind == "s":
                                kb.op("pe", MM(ps[ub][:, uc:uc + 256], kkm[:, km, hh, :], vx[0:128, ti, hh * 256:(hh + 1) * 256], True, True),
                                      reads=[("kkm", km), ("vx", ti)], writes=pk(ub), inc=(hh % 2 == 1))
                            else:
                                kb.op("pe", MM(ps[ub][:, uc:uc + 256], kk[j * C:(j + 1) * C, hh, :], vx[j * C:(j + 1) * C, ti, hh * 256:(hh + 1) * 256], True, True),
                                      reads=[("g_kk", 0), ("vx", ti)], writes=pk(ub), inc=(hh % 2 == 1))
                    if kind == "s":
                        emit_o(); emit_upd()
                    else:
                        emit_upd(); emit_o()
                    if not has_s and j == 0:
                        conv_part(2 * ti + 1, 15, CW)
                    if kind == "s" and j + 1 < nch:
                        kb.op("dve", TS(kkm[:, (j + 1) % 2, :, :], kk[:, :, :], seqmask[:, j + 1:j + 2], None, ALU.mult),
                              reads=[("g_kk", 0), ("seqmask", 0)], writes=[("kkm", (j + 1) % 2)])
                    for hh in range(NH):
                        ub = 1 + hh // 2; uc = (hh % 2) * 256
                        dec = expb[:, hh, (j + 1) * C - 1:(j + 1) * C]
                        if kind == "s":
                            kb.op("dve", STT(sst[:, hh, :], sst[:, hh, :], dec, ps[ub][:, uc:uc + 256], ALU.mult, ALU.add),
                                  reads=stgk(slot) + [("g_expb", 0)] + pk(ub), writes=stgk(slot))
                        else:
                            kb.op("dve", STT(S_f[:, l, hh, :], S_f[:, l, hh, :], dec, ps[ub][:, uc:uc + 256], ALU.mult, ALU.add),
                                  reads=[("S_f", l, hh), ("g_expb", 0)] + pk(ub), writes=[("S_f", l, hh)])
                    if kind == "s":
                        kb.dma("pool", ch_sto[slot], sgs[l, j].rearrange("h d v -> d h v"), sst, reads=stgk(slot))
                    else:
                        kb.op("act", ACTF(S_b[:, 1 - sp_, :, :], S_f[:, l, :, :], AF.Copy), reads=[("S_f", l, hh) for hh in range(NH)],
                              writes=[("S_b", 1 - sp_)])
                if kind != "s":
                    par["p"] = (par["p"] + nch) % 2
                for b_ in range(2):
                    kb.op("act", ACTF(osq[:, 4 * b_:4 * b_ + 4, 0:ntok], H4(ps[5 + b_][:, :])[:, :, 0:ntok], AF.Square),
                          reads=pk(5 + b_), writes=[("osq", b_)])
                if not has_s:
                    prefetch_diags()
                def epi2(ti=ti, c0=c0, ntok=ntok):
                    for hh in range(NH):
                        for e_ in range(2):
                            kb.op("pe", MM(ps[3][:, hh * 128:hh * 128 + ntok], ones_h[:], osq[:, hh * 2 + e_, 0:ntok], e_ == 0, e_ == 1),
                                  reads=[("osq", hh // 2), ("ones_h", 0)], writes=pk(3), inc=(hh == NH - 1 and e_ == 1))
                    rsqrt_ps(rstdh[:, :, 0:ntok], H4(ps[3][:, :])[:, :, 0:ntok], pk(3), [("rstdh", 0)])
                    for e_ in range(2):
                        kb.op("dve", TT(otmp[:, e_:8:2, 0:ntok], sgc[:, e_:8:2, c0:c0 + ntok], rstdh[:, :, 0:ntok], ALU.mult),
                              reads=[("sgc", c) for c in range(e_, 8, 2)] + [("rstdh", 0)], writes=[("otmp", e_)])
                    for b_ in range(2):
                        kb.op("dve", TT(ogla[:, 4 * b_:4 * b_ + 4, c0:c0 + ntok], H4(ps[5 + b_][:, :])[:, :, 0:ntok], otmp[:, 4 * b_:4 * b_ + 4, 0:ntok], ALU.mult),
                              reads=pk(5 + b_) + [("otmp", 0), ("otmp", 1)], writes=[("ogla", c) for c in range(4 * b_, 4 * b_ + 4)])

                pend_epi.append(epi2)
            while pend_epi:
                pend_epi.pop(0)()
            if has_s:
                for c in range(KC):
                    conv_chunk(c)
            rmsnorm_stats(cpre[:, :, 0:TB], [k_ for k in range(KC) for k_ in cpk(k)], TB)
            for c in range(KC):
                kb.op("dve", STT(cpre[:, c, 0:TB], cpre[:, c, 0:TB], vcol(128, l, c), rstd[:, 0:TB], ALU.mult, ALU.mult),
                      reads=cpk(c) + [("rstd", 0), ("vecs", 0)], writes=cpk(c))
                kb.op("act", ACTF(sgc[:, c, 0:TB], cpre[:, c, 0:TB], AF.Silu), reads=cpk(c), writes=[("sgc", c)])
            for i in range(4):
                wp, wpk = W.take()
                wa, wak = W.take()
                wb_, wbk = W.take()
                for jj in range(2):
                    c = i * 2 + jj
                    ba = next_bank(); bb = next_bank(); bo = next_bank()
                    for k in range(KC):
                        kb.op("pe", MM(ps[ba][:, 0:TB], wa[:, k, jj * 128:(jj + 1) * 128], xn[:, k, 0:TB], k == 0, k == KC - 1),
                              reads=[wak, ("xn", k)], writes=pk(ba), inc=(k == KC - 1))
                    for k in range(KC):
                        kb.op("pe", MM(ps[bb][:, 0:TB], wb_[:, k, jj * 128:(jj + 1) * 128], xn[:, k, 0:TB], k == 0, k == KC - 1),
                              reads=[wbk, ("xn", k)], writes=pk(bb), inc=(k == KC - 1))
                    for k in range(KC):
                        kb.op("pe", MM(ps[bo][:, 0:TB], wp[:, k, jj * 128:(jj + 1) * 128], sgc[:, k, 0:TB], k == 0, k == KC - 1),
                              reads=[wpk, ("sgc", k)], writes=pk(bo), inc=(k == KC - 1))
                    kb.op("act", ACTF(tmpA[:, 0:TB], ps[ba][:, 0:TB], AF.Sigmoid), reads=pk(ba), writes=[("tmp", 0)])
                    kb.op("act", ACTF(tmpB[:, 0:TB], ps[bb][:, 0:TB], AF.Sigmoid), reads=pk(bb), writes=[("tmp", 1)])
                    kb.op("dve", TT(tmpA[:, 0:TB], tmpA[:, 0:TB], ogla[:, c, 0:TB], ALU.mult), reads=[("tmp", 0), ("ogla", c)], writes=[("tmp", 0)])
                    kb.op("dve", TT(tmpB[:, 0:TB], tmpB[:, 0:TB], ps[bo][:, 0:TB], ALU.mult), reads=[("tmp", 1)] + pk(bo), writes=[("tmp", 1)])
                    kb.op("dve", TT(ogla[:, c, 0:TB], tmpA[:, 0:TB], tmpB[:, 0:TB], ALU.add), reads=[("tmp", 0), ("tmp", 1)], writes=[("ogla", c)])
            for i in range(4):
                wt, wk = W.take()
                for jj in range(2):
                    c = i * 2 + jj
                    bnk = next_bank()
                    for k in range(KC):
                        kb.op("pe", MM(ps[bnk][:, 0:TB], wt[:, k, jj * 128:(jj + 1) * 128], ogla[:, k, 0:TB], k == 0, k == KC - 1),
                              reads=[wk, ("ogla", k)], writes=pk(bnk), inc=(k == KC - 1))
                    kb.op("dve", TT(h[:, c, 0:TB], h[:, c, 0:TB], ps[bnk][:, 0:TB], ALU.add), reads=[hk(c)] + pk(bnk), writes=[hk(c)])
                    esq_chunk(c, TB, depth=3)
            if last:
                kb.dma("sp", ch_misc, sgp[l].rearrange("h d v -> d h v"), S_f[:, l, :, :], reads=[("S_f", l, hh) for hh in range(NH)])

        for bi, blk in enumerate(blocks):
            TB = blk["TB"]
            load_block_input(bi, blk)
            if stop == "load":
                break
            for l in range(nl):
                ffn(l, 0, TB, l > 0)
                if stop == "ffn1":
                    break
                mixer(l, bi, blk)
                if stop == "mixer":
                    break
                ffn(l, 64, TB, True)
            if stop is not None:
                break
            store_block_output(bi, blk)
        while pend_store:
            emit_store(pend_store.pop(0))
        assert stop is not None or W.i == len(slabs), (W.i, len(slabs))
        kb.wait_all("sp")
        kb.run(block)
    return nc


_NC_CACHE = {}


def kernel(**inputs):
    f32 = lambda a: np.ascontiguousarray(np.asarray(a, dtype=np.float32))
    x_prompt = f32(inputs["x_prompt"]); x_sample = f32(inputs["x_sample"])
    state_gla = f32(inputs["state_gla"]); cache_conv = f32(inputs["cache_conv"])
    consts = make_consts()
    cmat = np.stack([consts[n] for n in CONST_ORDER]).astype(np.float32)
    shared = {
        "meta": f32(inputs["meta_tokens"]),
        "norm_ffn1": f32(inputs["norm_ffn1"]), "wg1": f32(inputs["w_ffn1_gate"]), "wu1": f32(inputs["w_ffn1_up"]), "wd1": f32(inputs["w_ffn1_down"]),
        "norm_mix": f32(inputs["norm_mix"]), "w_in": f32(inputs["w_in"]), "w_dec": f32(inputs["w_decay_up"]), "b_dec": f32(inputs["b_decay"]),
        "gla_norm": f32(inputs["gla_norm"]), "w_dw": f32(inputs["w_dw"]), "b_dw": f32(inputs["b_dw"]), "conv_norm": f32(inputs["conv_norm"]),
        "w_pw": f32(inputs["w_pw"]), "w_out": f32(inputs["w_out"]),
        "norm_ffn2": f32(inputs["norm_ffn2"]), "wg2": f32(inputs["w_ffn2_gate"]), "wu2": f32(inputs["w_ffn2_up"]), "wd2": f32(inputs["w_ffn2_down"]),
        "norm_final": f32(inputs["norm_final"]).reshape(1, D),
        "cmat": cmat, "seqmask": consts["seqmask"],
    }
    n = 8
    in_maps = []
    for c in range(n):
        m = dict(shared)
        m["xp"] = x_prompt[c]
        m["xs"] = x_sample[c * NSEQ:(c + 1) * NSEQ].reshape(NSEQ * DSEQ, D)
        m["sgi"] = np.ascontiguousarray(state_gla[:, c * NSEQ:(c + 1) * NSEQ])
        m["cci"] = np.ascontiguousarray(cache_conv[:, c * NSEQ:(c + 1) * NSEQ])
        in_maps.append(m)
    if "nc" not in _NC_CACHE:
        _NC_CACHE["nc"] = build_program()
    nc = _NC_CACHE["nc"]
    res = run_bass_kernel_spmd(nc, in_maps, core_ids=list(range(n)))
    r = res.results
    y_prompt = np.stack([r[c]["yp"] for c in range(n)], axis=0)
    y_sample = np.concatenate([r[c]["ys"].reshape(NSEQ, DSEQ, D) for c in range(n)], axis=0)
    sg_p = np.stack([r[c]["sgp"] for c in range(n)], axis=1)
    cc_p = np.stack([r[c]["ccp"] for c in range(n)], axis=1)
    sg_s = np.concatenate([r[c]["sgs"] for c in range(n)], axis=1)
    cc_s = np.concatenate([r[c]["ccs"] for c in range(n)], axis=1)
    return (y_prompt.astype(np.float32), y_sample.astype(np.float32), sg_p.astype(np.float32), cc_p.astype(np.float32),
            sg_s.astype(np.float32), cc_s.astype(np.float32))
```

```python
import contextlib
import numpy as np
import concourse.bass as bass
import concourse.mybir as mybir
from concourse.bass_utils import run_bass_kernel_spmd

F32 = mybir.dt.float32
BF16 = mybir.dt.bfloat16
AF = mybir.ActivationFunctionType
ALU = mybir.AluOpType

D = 1024; KC = 8; DFF = 2816; FC = 22; DIN = 7184; NL = 4; NH = 4; DK = 128; DV = 256
NSEQ = 16; DSEQ = 8; NMETA = 16; SEQ = 2048; CW = 31; HIST = 30
EPS = 1e-6
TBM = 448
import os as _os
NOCACHE = _os.environ.get("NOCACHE") == "1"
OQ, OK_, OV, OG, OA, OGA, OGB, OMA, OMB = 0, 512, 1024, 2048, 3072, 3088, 4112, 5136, 6160


class Stream:
    def __init__(self, name, sem):
        self.name = name; self.sem = sem; self.count = 0; self.ops = []; self.known = {}; self.trace = []


class Chan:
    def __init__(self, name, sem):
        self.name = name; self.sem = sem; self.count = 0


class RS:
    __slots__ = ("w", "r")

    def __init__(self):
        self.w = None; self.r = {}


class KB:
    def __init__(self, nc, ctx):
        self.nc = nc; self.ctx = ctx
        self.streams = {}
        for n in ("pe", "dve", "act", "pool", "sp"):
            self.streams[n] = Stream(n, ctx.enter_context(nc.semaphore("s_" + n)))
        self.res = {}
        self.chans = []

    def chan(self, name):
        c = Chan(name, self.ctx.enter_context(self.nc.semaphore("c_" + name)))
        self.chans.append(c)
        return c

    def _st(self, key):
        s = self.res.get(key)
        if s is None:
            s = self.res[key] = RS()
        return s

    def _deps(self, reads, writes):
        deps = []
        for k in reads:
            s = self._st(k)
            if s.w is not None:
                deps.append(s.w)
        for k in writes:
            s = self._st(k)
            if s.w is not None:
                deps.append(s.w)
            deps.extend(s.r.values())
        return deps

    def _emit_waits(self, st, deps, skip_self=False):
        need = {}
        for (sem, val, owner) in deps:
            if skip_self and owner == st.name:
                continue
            if st.known.get(id(sem), 0) >= val:
                continue
            if need.get(id(sem), (None, 0))[1] < val:
                need[id(sem)] = (sem, val)
        for sem, val in need.values():
            st.known[id(sem)] = val
            st.ops.append(lambda e, sem=sem, val=val: e.wait_ge(sem, val))
            st.trace.append(("w", id(sem), val))

    def _mark(self, ticket, reads, writes, who):
        for k in reads:
            self._st(k).r[who] = ticket
        for k in writes:
            s = self._st(k); s.w = ticket; s.r = {}

    def op(self, eng, fn, reads=(), writes=(), inc=True):
        st = self.streams[eng]
        deps = self._deps(reads, writes)
        self._emit_waits(st, deps, skip_self=(eng == "pe"))
        if inc:
            st.count += 1
            sem = st.sem
            st.ops.append(lambda e, fn=fn, sem=sem: fn(e).then_inc(sem, 1))
            st.trace.append(("i", id(sem), 1))
            ticket = (st.sem, st.count, st.name)
        else:
            st.ops.append(lambda e, fn=fn: fn(e))
            ticket = (st.sem, st.count + 1, st.name)
        self._mark(ticket, reads, writes, eng)
        return ticket

    def dma(self, q, chan, out, in_, reads=(), writes=(), cont=False):
        st = self.streams[q]
        deps = self._deps(reads, writes)
        if chan.count > 0 and not cont:
            deps.append((chan.sem, 16 * chan.count, "dma"))
        self._emit_waits(st, deps)
        chan.count += 1
        sem = chan.sem
        st.ops.append(lambda e, out=out, in_=in_, sem=sem: e.dma_start(out=out, in_=in_).then_inc(sem, 16))
        st.trace.append(("i", id(sem), 16))
        ticket = (chan.sem, 16 * chan.count, "dma:" + chan.name)
        self._mark(ticket, reads, writes, "dma:" + chan.name)
        return ticket

    def wait_all(self, eng):
        st = self.streams[eng]
        for s in self.streams.values():
            if s.count > 0 and s is not st:
                st.ops.append(lambda e, sem=s.sem, v=s.count: e.wait_ge(sem, v))
        for c in self.chans:
            if c.count > 0:
                st.ops.append(lambda e, sem=c.sem, v=16 * c.count: e.wait_ge(sem, v))

    def run(self, block):
        ss = self.streams

        @block.tensor
        def _(e):
            for f in ss["pe"].ops:
                f(e)

        @block.vector
        def _(e):
            for f in ss["dve"].ops:
                f(e)

        @block.scalar
        def _(e):
            for f in ss["act"].ops:
                f(e)

        @block.gpsimd
        def _(e):
            for f in ss["pool"].ops:
                f(e)

        @block.sync
        def _(e):
            for f in ss["sp"].ops:
                f(e)


def MM(out, lhsT, rhs, start, stop):
    return lambda e: e.matmul(out, lhsT=lhsT, rhs=rhs, start=start, stop=stop)


def TR(out, in_, ident):
    return lambda e: e.transpose(out, in_, ident)


def ACTF(out, in_, func, bias=None, scale=None):
    kw = {}
    if bias is not None:
        kw["bias"] = bias
    if scale is not None:
        kw["scale"] = scale
    return lambda e: e.activation(out=out, in_=in_, func=func, **kw)


def TT(out, in0, in1, op):
    return lambda e: e.tensor_tensor(out=out, in0=in0, in1=in1, op=op)


def TS(out, in0, s1, s2, op0, op1=None):
    if op1 is None:
        return lambda e: e.tensor_scalar(out=out, in0=in0, scalar1=s1, scalar2=None, op0=op0)
    return lambda e: e.tensor_scalar(out=out, in0=in0, scalar1=s1, scalar2=s2, op0=op0, op1=op1)


def STT(out, in0, scalar, in1, op0, op1):
    return lambda e: e.scalar_tensor_tensor(out=out, in0=in0, scalar=scalar, in1=in1, op0=op0, op1=op1)


def CP(out, in_):
    return lambda e: e.tensor_copy(out=out, in_=in_)


def RCP(out, in_):
    return lambda e: e.reciprocal(out=out, in_=in_)


def MS(ap, v):
    return lambda e: e.memset(ap, v)


def make_consts():
    c = {}
    c["ident"] = np.eye(128, dtype=np.float32)
    for C in (64, 8, 16):
        s = np.arange(128)[:, None]; t = np.arange(128)[None, :]
        m = ((s // C) == (t // C)) & (s <= t)
        c["M%d" % C] = m.astype(np.float32)
        c["U%d" % C] = (m.astype(np.float32) * (-1.0 / 16.0)).astype(np.float32)
        r = ((s // C) == (t // C)) & (s > t)
        c["R%d" % C] = (r.astype(np.float32) * (-1.0 / 16.0)).astype(np.float32)
    sm = (np.arange(128)[:, None] // 8 == np.arange(16)[None, :]).astype(np.float32)
    c["seqmask"] = sm
    return c


CONST_ORDER = ["ident", "M64", "M8", "M16", "U64", "U8", "U16", "R64", "R8", "R16"]


def block_defs():
    blocks = []
    blocks.append(dict(TB=400, tiles=[("s", 0, 128, 8), ("m", 128, 16, 16), ("p", 144, 128, 64), ("p", 272, 128, 64)],
                       pc0=128, NP=272, real0=0, nreal=256, realcol=144))
    r = 256
    for b in range(4):
        blocks.append(dict(TB=448, tiles=[("p", 0, 128, 64), ("p", 128, 128, 64), ("p", 256, 128, 64), ("p", 384, 64, 64)],
                           pc0=0, NP=448, real0=r, nreal=448, realcol=0))
        r += 448
    return blocks


def build_program(nb=5, nl=NL, stop=None):
    nc = bass.Bass("TRN2", target_bir_lowering=False)

    def din(name, shape):
        return nc.dram_tensor(name, list(shape), F32, kind="ExternalInput").ap()

    def dout(name, shape):
        return nc.dram_tensor(name, list(shape), F32, kind="ExternalOutput").ap()

    xp = din("xp", [SEQ, D]); xs = din("xs", [NSEQ * DSEQ, D])
    sgi = din("sgi", [NL, NSEQ, NH, DK, DV]); cci = din("cci", [NL, NSEQ, HIST, D])
    meta = din("meta", [NMETA, D])
    norm_ffn1 = din("norm_ffn1", [NL, D]); wg1 = din("wg1", [NL, D, DFF]); wu1 = din("wu1", [NL, D, DFF]); wd1 = din("wd1", [NL, DFF, D])
    norm_mix = din("norm_mix", [NL, D]); w_in = din("w_in", [NL, D, DIN]); w_dec = din("w_dec", [NL, 16, 512]); b_dec = din("b_dec", [NL, 512])
    gla_norm = din("gla_norm", [NL, D]); w_dw = din("w_dw", [NL, CW, D]); b_dw = din("b_dw", [NL, D]); conv_norm = din("conv_norm", [NL, D])
    w_pw = din("w_pw", [NL, D, D]); w_out = din("w_out", [NL, D, D])
    norm_ffn2 = din("norm_ffn2", [NL, D]); wg2 = din("wg2", [NL, D, DFF]); wu2 = din("wu2", [NL, D, DFF]); wd2 = din("wd2", [NL, DFF, D])
    norm_final = din("norm_final", [1, D])
    cmat = din("cmat", [len(CONST_ORDER), 128, 128]); seqmask_d = din("seqmask", [128, 16])

    yp = dout("yp", [SEQ, D]); ys = dout("ys", [NSEQ * DSEQ, D])
    sgp = dout("sgp", [NL, NH, DK, DV]); ccp = dout("ccp", [NL, HIST, D])
    sgs = dout("sgs", [NL, NSEQ, NH, DK, DV]); ccs = dout("ccs", [NL, NSEQ, HIST, D])

    blocks = block_defs()[:nb]

    with contextlib.ExitStack() as ctx:
        kb = KB(nc, ctx)

        def sb(name, shape, dt):
            return ctx.enter_context(nc.sbuf_tensor(name, list(shape), dt))

        h = sb("h", [128, KC, TBM], F32)
        S_f = sb("S_f", [128, NL, NH, DV], F32)
        S_b = sb("S_b", [128, 2, NH, DV], BF16)
        hist = sb("hist", [128, NL, KC, HIST], BF16)
        vecs = sb("vecs", [128, 256], F32)
        wdwT = sb("wdwT", [128, NL, 256], F32)
        cm = sb("cm", [128, len(CONST_ORDER), 128], F32)
        seqmask = sb("seqmaskt", [128, 16], F32)
        ones_b = sb("ones_b", [128, 128], BF16)
        ones_h = sb("ones_h", [128, 128], BF16)
        wdaug = sb("wdaug", [128, 512], F32)
        xn = sb("xn", [128, KC, TBM], BF16)
        vx = sb("vx", [128, 4, 1024], BF16)
        xsq = vx[:].rearrange("p a b -> p (a b)")[:, 0:KC * TBM].rearrange("p (k t) -> p k t", t=TBM)
        rstd = sb("rstd", [128, TBM], F32)
        A1 = sb("A1", [128, FC * TBM // 2], F32)
        act = A1[:].bitcast(BF16).rearrange("p (j t) -> p j t", t=TBM)
        tmpA = sb("tmpA", [128, TBM], F32)
        tmpB = sb("tmpB", [128, TBM], F32)
        alr = sb("alr", [128, TBM], F32)
        Ltok = sb("Ltok", [128, 2, 512], F32)
        qk = sb("qk", [128, 8, TBM], F32)
        sgc = sb("sgc", [128, KC, TBM], BF16)
        uext_p = sb("uext_p", [128, KC, HIST + TBM], BF16)
        uext_s = sb("uext_s", [128, KC, NSEQ, HIST + DSEQ], BF16)
        ogla = sb("ogla", [128, KC, TBM], BF16)
        expb = sb("expb", [128, NH, 128], F32)
        expnb = sb("expnb", [128, NH, 128], F32)
        expd = sb("expd", [128, NH, 128], F32)
        kkT = sb("kkT", [128, NH, 128], F32)
        qt = sb("qt", [128, NH, 128], BF16)
        kt = sb("kt", [128, NH, 128], BF16)
        kk = sb("kk", [128, NH, 128], BF16)
        scm = sb("scm", [128, NH, 128], BF16)
        kkm = sb("kkm", [128, 2, NH, 128], BF16)
        osq = sb("osq", [128, 8, 128], BF16)
        rstdh = sb("rstdh", [128, NH, 128], F32)
        otmp = sb("otmp", [128, 8, 128], F32)
        NDIAG = 16
        diag = sb("diag", [128, NDIAG, 128], BF16)
        NSA = 6; NSB = 3
        ringA = [sb("ringA%d" % i, [128, KC, 256], BF16) for i in range(NSA)]
        ringB = [sb("ringB%d" % i, [128, FC, 128], BF16) for i in range(NSB)]
        chA = [kb.chan("ra%d" % i) for i in range(NSA)]
        chB = [kb.chan("rb%d" % i) for i in range(NSB)]
        ps = [ctx.enter_context(nc.psum_tensor("ps%d" % i, [128, 512], F32)) for i in range(8)]

        ch_init = kb.chan("init")
        ch_x = [kb.chan("x0"), kb.chan("x1")]
        ch_y = [kb.chan("y0"), kb.chan("y1")]
        ch_st = [kb.chan("st%d" % i) for i in range(4)]
        ch_sto = [kb.chan("sto%d" % i) for i in range(4)]
        ch_cc = kb.chan("cc")
        ch_cc2 = [kb.chan("cca"), kb.chan("ccb")]
        ch_misc = kb.chan("misc")
        ch_wd = kb.chan("wd")
        ch_dd = kb.chan("dd")

        block = ctx.enter_context(nc.Block())

        ident = cm[:, 0, :]
        Mmask = {64: cm[:, 1, :], 8: cm[:, 2, :], 16: cm[:, 3, :]}
        Umat = {64: cm[:, 4, :], 8: cm[:, 5, :], 16: cm[:, 6, :]}
        Rmat = {64: cm[:, 7, :], 8: cm[:, 8, :], 16: cm[:, 9, :]}

        def pk(b, c0=0, c1=512):
            return [("ps", b)]

        GR = TBM * 2

        def a1keys(byte_off, nbytes):
            return [("A1", g) for g in range(byte_off // GR, (byte_off + nbytes - 1) // GR + 1)]

        SST = 5 * GR // 4

        def stg(slot):
            return A1[:, slot * SST:slot * SST + 1024]

        def stgk(slot):
            return a1keys(slot * SST * 4, 4096)

        def vxkeys_xsq(k0, k1):
            lo = (k0 * TBM) // 1024; hi = (k1 * TBM - 1) // 1024
            return [("vx", t) for t in range(lo, hi + 1)]

        kb.dma("sp", ch_init, cm[:], cmat.rearrange("c p n -> p c n"), writes=[("cm", 0)])
        kb.dma("sp", ch_init, seqmask[:], seqmask_d[:, :], writes=[("seqmask", 0)])
        kb.op("dve", MS(ones_b[:], 1.0 / 1024.0), writes=[("ones_b", 0)])
        kb.op("dve", MS(ones_h[:], 1.0 / 256.0), writes=[("ones_h", 0)])
        kb.op("dve", MS(alr[:], 0.0), writes=[("alr", 0)])
        kb.op("dve", MS(alr[32:33, :], 1.0), writes=[("alr", 0)])
        kb.op("dve", MS(wdaug[:], 0.0), writes=[("wdaug", 0)])
        kb.op("dve", MS(S_f[:].rearrange("p l h v -> p (l h v)"), 0.0), writes=[("S_f", l, hh) for l in range(NL) for hh in range(NH)])
        kb.op("dve", MS(hist[:].rearrange("p l k t -> p (l k t)"), 0.0), writes=[("hist", l) for l in range(NL)])

        s0 = stg(0)
        for i, v_ in enumerate((norm_ffn1, norm_mix, norm_ffn2, gla_norm)):
            kb.dma("sp", ch_misc, s0[32 * i:32 * (i + 1), 0:128], v_.rearrange("l (k p) -> (l k) p", p=128), writes=stgk(0), cont=(i > 0))
        kb.op("pe", TR(ps[4][:, 0:128], s0[:, 0:128], ident), reads=stgk(0) + [("cm", 0)], writes=pk(4))
        kb.op("dve", CP(vecs[:, 0:128], ps[4][:, 0:128]), reads=pk(4), writes=[("vecs", 0)])
        s1 = stg(1)
        kb.dma("sp", ch_misc, s1[0:32, 0:128], conv_norm.rearrange("l (k p) -> (l k) p", p=128), writes=stgk(1))
        kb.dma("sp", ch_misc, s1[32:64, 0:128], b_dw.rearrange("l (k p) -> (l k) p", p=128), writes=stgk(1), cont=True)
        kb.dma("sp", ch_misc, s1[64:72, 0:128], norm_final.rearrange("l (k p) -> (l k) p", p=128), writes=stgk(1), cont=True)
        kb.op("pe", TR(ps[4][:, 128:200], s1[0:72, 0:128], ident[0:72, 0:72]), reads=stgk(1) + [("cm", 0)], writes=pk(4))
        kb.op("dve", CP(vecs[:, 128:200], ps[4][:, 128:200]), reads=pk(4), writes=[("vecs", 0)])
        for l in range(NL):
            sl = stg(2 + (l % 2))
            src = w_dw[l].rearrange("j (k p) -> (j k) p", p=128)
            kb.dma("sp", ch_misc, sl[0:128, 0:128], src[0:128, :], writes=stgk(2 + (l % 2)))
            kb.dma("sp", ch_misc, sl[0:120, 128:256], src[128:248, :], writes=stgk(2 + (l % 2)), cont=True)
            kb.op("pe", TR(ps[5][:, 0:128], sl[0:128, 0:128], ident), reads=stgk(2 + (l % 2)) + [("cm", 0)], writes=pk(5))
            kb.op("pe", TR(ps[5][:, 128:248], sl[0:120, 128:256], ident[0:120, 0:120]), reads=stgk(2 + (l % 2)), writes=pk(5))
            kb.op("dve", CP(wdwT[:, l, 0:248], ps[5][:, 0:248]), reads=pk(5), writes=[("wdwT", l)])

        def vcol(base, l, k):
            c = base + l * 8 + k
            return vecs[:, c:c + 1]

        slabs = []

        def add_slab(ring, ap, nk, ncols):
            slabs.append((ring, ap, nk, ncols))
            return len(slabs) - 1

        state = dict(next_load=0, cntA=0, cntB=0, cur=0)
        slot_of = {}
        prev_occ = {}
        last_in_slot = {}

        def plan_slots():
            ca = cb = 0
            for i, (ring, ap, nk, ncols) in enumerate(slabs):
                if ring == "A":
                    s = ("A", ca % NSA); ca += 1
                else:
                    s = ("B", cb % NSB); cb += 1
                slot_of[i] = s
                prev_occ[i] = last_in_slot.get(s, -1)
                last_in_slot[s] = i

        pend_store = []

        def scr_view(idx, nk, ncols):
            return wscr[:, scr_off[idx]:scr_off[idx] + nk * ncols].rearrange("p (k n) -> p k n", n=ncols)

        def emit_store(i):
            ring, ap, nk, ncols = slabs[i]
            rname, s = slot_of[i]
            kb.dma("pool", chSA[s], scr_view(i % per_pass, nk, ncols), ringA[s][:, 0:nk, 0:ncols],
                   reads=[("ringA", s)], writes=[("scr", i % per_pass)])

        def emit_loads(cur):
            while state["next_load"] < len(slabs) and (state["next_load"] <= cur + 2 or prev_occ[state["next_load"]] < cur):
                i = state["next_load"]
                ring, ap, nk, ncols = slabs[i]
                rname, s = slot_of[i]
                if rname == "A":
                    dst = ringA[s][:, 0:nk, 0:ncols]; ch = chA[s]
                else:
                    dst = ringB[s][:, 0:nk, 0:ncols]; ch = chB[s]
                p_ = i // per_pass
                idx = i % per_pass
                if idx == 0:
                    while pend_store:
                        emit_store(pend_store.pop(0))
                cp_ = cache_pass[idx]
                for pi in [q_ for q_ in pend_store if slot_of[q_] == slot_of[i]]:
                    pend_store.remove(pi)
                    emit_store(pi)
                if cp_ is None or p_ <= cp_:
                    kb.dma("pool", ch, dst, ap.rearrange("(k p) n -> p k n", p=128), writes=[("ring" + rname, s)])
                    if cp_ is not None and p_ == cp_:
                        pend_store.append(i)
                        if len(pend_store) > 3:
                            emit_store(pend_store.pop(0))
                else:
                    kb.dma("pool", ch, dst, scr_view(idx, nk, ncols), reads=[("scr", idx)], writes=[("ring" + rname, s)])
                state["next_load"] += 1

        class SlabIter:
            def __init__(self):
                self.i = 0

            def take(self):
                i = self.i; self.i += 1
                emit_loads(i - 2)
                rname, s = slot_of[i]
                t = ringA[s] if rname == "A" else ringB[s]
                return t, ("ring" + rname, s)

        def plan_layer(l):
            def ffn(wg, wu, wd):
                for j2 in range(0, FC, 2):
                    add_slab("A", wg[l][:, j2 * 128:(j2 + 2) * 128], KC, 256)
                    add_slab("A", wu[l][:, j2 * 128:(j2 + 2) * 128], KC, 256)
                for jo in range(KC):
                    add_slab("B", wd[l][:, jo * 128:(jo + 1) * 128], FC, 128)
            ffn(wg1, wu1, wd1)
            wi = w_in[l]
            add_slab("A", wi[:, OA:OA + 16], KC, 16)
            for o in (OQ, OQ + 256, OK_, OK_ + 256):
                add_slab("A", wi[:, o:o + 256], KC, 256)
            for i in range(4):
                add_slab("A", wi[:, OV + 256 * i:OV + 256 * (i + 1)], KC, 256)
            for i in range(4):
                add_slab("A", wi[:, OG + 256 * i:OG + 256 * (i + 1)], KC, 256)
            for i in range(4):
                add_slab("A", wi[:, OGA + 256 * i:OGA + 256 * (i + 1)], KC, 256)
                add_slab("A", wi[:, OGB + 256 * i:OGB + 256 * (i + 1)], KC, 256)
            for i in range(4):
                add_slab("A", w_pw[l][:, 256 * i:256 * (i + 1)], KC, 256)
                add_slab("A", wi[:, OMA + 256 * i:OMA + 256 * (i + 1)], KC, 256)
                add_slab("A", wi[:, OMB + 256 * i:OMB + 256 * (i + 1)], KC, 256)
            for i in range(4):
                add_slab("A", w_out[l][:, 256 * i:256 * (i + 1)], KC, 256)
            ffn(wg2, wu2, wd2)

        for b in range(len(blocks)):
            for l in range(nl):
                plan_layer(l)
        plan_slots()
        per_pass = len(slabs) // len(blocks)
        scr_off = []
        cache_pass = []
        tot = 0
        na_ = 0
        for i in range(per_pass):
            scr_off.append(tot)
            if slabs[i][0] == "A" and len(blocks) > 1 and not NOCACHE:
                tot += slabs[i][2] * slabs[i][3]
                cache_pass.append(0)
                na_ += 1
            else:
                cache_pass.append(None)
        tot = max(tot, 16)
        wscr = nc.dram_tensor("wscr", [128, tot], BF16, kind="Internal").ap()
        chSA = [kb.chan("sa%d" % i) for i in range(NSA)]
        W = SlabIter()

        mb = dict(i=0)

        def next_bank():
            b = mb["i"] % 4; mb["i"] += 1
            return b

        hk = lambda k: ("h", k)


        def rmsnorm_stats(src, src_keys, TB):
            kb.op("act", ACTF(xsq[:, :, 0:TB], src, AF.Square), reads=src_keys, writes=[("vx", t) for t in range(4)])
            for k in range(KC):
                kb.op("pe", MM(ps[4][:, 0:TB], ones_b[:], xsq[:, k, 0:TB], k == 0, k == KC - 1),
                      reads=[("vx", t) for t in range(4)] + [("ones_b", 0)], writes=pk(4), inc=(k == KC - 1))
            rsqrt_ps(rstd[:, 0:TB], ps[4][:, 0:TB], pk(4), [("rstd", 0)])

        esq = dict(pending=[], n=0)

        def esq_mm(k, TB, stop):
            kb.op("pe", MM(ps[4][:, 0:TB], ones_b[:], xsq[:, k, 0:TB], esq["n"] == 0, stop),
                  reads=vxkeys_xsq(k, k + 1) + [("ones_b", 0)], writes=pk(4), inc=stop)
            esq["n"] += 1

        def esq_chunk(k, TB, depth=1):
            kb.op("act", ACTF(xsq[:, k, 0:TB], h[:, k, 0:TB], AF.Square), reads=[hk(k)], writes=vxkeys_xsq(k, k + 1))
            esq["pending"].append(k)
            while len(esq["pending"]) > depth:
                esq_mm(esq["pending"].pop(0), TB, False)

        def rsqrt_ps(dst, src_ps, rkeys, wkeys):
            kb.op("act", ACTF(dst, src_ps, AF.Ln, bias=EPS), reads=rkeys, writes=wkeys)
            kb.op("act", ACTF(dst, dst, AF.Exp, scale=-0.5), reads=wkeys, writes=wkeys)

        def esq_finish(TB):
            while esq["pending"]:
                k = esq["pending"].pop(0)
                esq_mm(k, TB, len(esq["pending"]) == 0)
            esq["n"] = 0
            rsqrt_ps(rstd[:, 0:TB], ps[4][:, 0:TB], pk(4), [("rstd", 0)])

        def rmsnorm_to_xn(l, base, TB, pre=False):
            if pre:
                esq_finish(TB)
            else:
                rmsnorm_stats(h[:, :, 0:TB], [hk(k) for k in range(KC)], TB)
            for k in range(KC):
                kb.op("dve", STT(xn[:, k, 0:TB], h[:, k, 0:TB], vcol(base, l, k), rstd[:, 0:TB], ALU.mult, ALU.mult),
                      reads=[hk(k), ("rstd", 0), ("vecs", 0)], writes=[("xn", k)])

        def ffn(l, base, TB, pre):
            rmsnorm_to_xn(l, base, TB, pre)
            for j2 in range(0, FC, 2):
                wgt, wgk = W.take()
                wut, wuk = W.take()
                banks = [(next_bank(), next_bank()) for jj in range(2)]
                if j2 == 0:
                    for k in range(KC):
                        for jj in range(2):
                            bg, bu = banks[jj]
                            kb.op("pe", MM(ps[bg][:, 0:TB], wgt[:, k, jj * 128:(jj + 1) * 128], xn[:, k, 0:TB], k == 0, k == KC - 1),
                                  reads=[wgk, ("xn", k)], writes=pk(bg), inc=(k == KC - 1))
                            kb.op("pe", MM(ps[bu][:, 0:TB], wut[:, k, jj * 128:(jj + 1) * 128], xn[:, k, 0:TB], k == 0, k == KC - 1),
                                  reads=[wuk, ("xn", k)], writes=pk(bu), inc=(k == KC - 1))
                for jj in range(2):
                    j = j2 + jj
                    bg, bu = banks[jj]
                    if j2 != 0:
                        for k in range(KC):
                            kb.op("pe", MM(ps[bg][:, 0:TB], wgt[:, k, jj * 128:(jj + 1) * 128], xn[:, k, 0:TB], k == 0, k == KC - 1),
                                  reads=[wgk, ("xn", k)], writes=pk(bg), inc=(k == KC - 1))
                        for k in range(KC):
                            kb.op("pe", MM(ps[bu][:, 0:TB], wut[:, k, jj * 128:(jj + 1) * 128], xn[:, k, 0:TB], k == 0, k == KC - 1),
                                  reads=[wuk, ("xn", k)], writes=pk(bu), inc=(k == KC - 1))
                    tmp = tmpA if (j % 2 == 0) else tmpB
                    tk = ("tmp", j % 2)
                    kb.op("act", ACTF(tmp[:, 0:TB], ps[bg][:, 0:TB], AF.Silu), reads=pk(bg), writes=[tk])
                    kb.op("dve", TT(act[:, j, 0:TB], tmp[:, 0:TB], ps[bu][:, 0:TB], ALU.mult), reads=[tk] + pk(bu), writes=[("A1", j)])
            for jo in range(KC):
                wdt, wdk = W.take()
                bo = next_bank()
                for j in range(FC):
                    kb.op("pe", MM(ps[bo][:, 0:TB], wdt[:, j, 0:128], act[:, j, 0:TB], j == 0, j == FC - 1),
                          reads=[wdk, ("A1", j)], writes=pk(bo), inc=(j == FC - 1))
                kb.op("dve", STT(h[:, jo, 0:TB], ps[bo][:, 0:TB], 0.5, h[:, jo, 0:TB], ALU.mult, ALU.add),
                      reads=pk(bo) + [hk(jo)], writes=[hk(jo)])
                esq_chunk(jo, TB)

        def load_block_input(bi, blk):
            for ti, (kind, c0, ntok, C) in enumerate(blk["tiles"]):
                slot = ti % 2
                st_ = stg(slot)
                if kind == "s":
                    src = xs[0:128, :]
                elif kind == "m":
                    src = meta[0:16, :]
                else:
                    r0 = blk["real0"] + (c0 - blk["realcol"])
                    src = xp[r0:r0 + ntok, :]
                kb.dma("sp", ch_x[slot], st_[0:ntok, :], src, writes=stgk(slot))
                for half in range(2):
                    pb = ps[5 + half]
                    for kk_ in range(4):
                        k = half * 4 + kk_
                        kb.op("pe", TR(pb[:, kk_ * 128:kk_ * 128 + ntok], st_[0:ntok, k * 128:(k + 1) * 128], ident[0:ntok, 0:ntok]),
                              reads=stgk(slot) + [("cm", 0)], writes=pk(5 + half), inc=(kk_ == 3))
                    src_v = pb[:, :].rearrange("p (a b) -> p a b", b=128)[:, :, 0:ntok]
                    kb.op("act" if half == 0 else "dve",
                          (ACTF(h[:, half * 4:half * 4 + 4, c0:c0 + ntok], src_v, AF.Copy) if half == 0
                           else CP(h[:, half * 4:half * 4 + 4, c0:c0 + ntok], src_v)),
                          reads=pk(5 + half), writes=[hk(k) for k in range(half * 4, half * 4 + 4)])

        def store_block_output(bi, blk):
            TB = blk["TB"]
            esq_finish(TB)
            for k in range(KC):
                kb.op("dve", STT(h[:, k, 0:TB], h[:, k, 0:TB], vecs[:, 192 + k:193 + k], rstd[:, 0:TB], ALU.mult, ALU.mult),
                      reads=[hk(k), ("rstd", 0), ("vecs", 0)], writes=[hk(k)])
            for ti, (kind, c0, ntok, C) in enumerate(blk["tiles"]):
                if kind == "m":
                    continue
                slot = 2 + (ti % 2)
                st_ = stg(slot)
                for half in range(2):
                    pb = ps[5 + half]
                    for kk_ in range(4):
                        k = half * 4 + kk_
                        kb.op("pe", TR(pb[0:ntok, kk_ * 128:(kk_ + 1) * 128], h[:, k, c0:c0 + ntok], ident),
                              reads=[hk(k), ("cm", 0)], writes=pk(5 + half), inc=(kk_ == 3))
                    kb.op("act" if half == 0 else "dve",
                          (ACTF(st_[0:ntok, half * 512:(half + 1) * 512], pb[0:ntok, :], AF.Copy) if half == 0
                           else CP(st_[0:ntok, half * 512:(half + 1) * 512], pb[0:ntok, :])),
                          reads=pk(5 + half), writes=stgk(slot))
                if kind == "s":
                    dst = ys[0:128, :]
                else:
                    r0 = blk["real0"] + (c0 - blk["realcol"])
                    dst = yp[r0:r0 + ntok, :]
                kb.dma("sp", ch_y[ti % 2], dst, st_[0:ntok, :], reads=stgk(slot))

        def mixer(l, bi, blk):
            TB = blk["TB"]; tiles = blk["tiles"]; NP = blk["NP"]; pc0 = blk["pc0"]
            has_s = (bi == 0)
            last = (bi == 4)
            rmsnorm_to_xn(l, 32, TB, True)
            kb.dma("sp", ch_wd, wdaug[0:16, :], w_dec[l], writes=[("wdaug", 0)])
            kb.dma("sp", ch_wd, wdaug[32:33, :], b_dec[l:l + 1, :], writes=[("wdaug", 0)], cont=True)
            if bi == 0:
                kb.op("dve", MS(uext_p[:, :, 0:HIST], 0.0), writes=[("uext_p", k) for k in range(KC)])
            else:
                kb.op("dve", CP(uext_p[:, :, 0:HIST], hist[:, l, :, :]), reads=[("hist", l)], writes=[("uext_p", k) for k in range(KC)])
            par = dict(p=0)
            kb.op("act", ACTF(S_b[:, 0, :, :], S_f[:, l, :, :], AF.Copy), reads=[("S_f", l, hh) for hh in range(NH)],
                  writes=[("S_b", 0)])

            wt_a, wk_a = W.take()
            bnk_a = next_bank()
            wt_q, wk_q = W.take()
            bq = [next_bank(), next_bank()]
            for k in range(KC):
                kb.op("pe", MM(ps[bnk_a][0:16, 0:TB], wt_a[:, k, 0:16], xn[:, k, 0:TB], k == 0, k == KC - 1),
                      reads=[wk_a, ("xn", k)], writes=pk(bnk_a), inc=(k == KC - 1))
                for jj in range(2):
                    kb.op("pe", MM(ps[bq[jj]][:, 0:TB], wt_q[:, k, jj * 128:(jj + 1) * 128], xn[:, k, 0:TB], k == 0, k == KC - 1),
                          reads=[wk_q, ("xn", k)], writes=pk(bq[jj]), inc=(k == KC - 1))
            kb.op("dve", CP(alr[0:16, 0:TB], ps[bnk_a][0:16, 0:TB]), reads=pk(bnk_a), writes=[("alr", 0)])
            def emit_z(tj):
                _, c0z, ntz, _ = tiles[tj]
                ltz = Ltok[:, tj % 2, :]
                ltkz = ("Ltok", tj % 2)
                kb.op("pe", MM(ps[0][0:ntz, 0:512], alr[0:33, c0z:c0z + ntz], wdaug[0:33, :], True, True),
                      reads=[("alr", 0), ("wdaug", 0)], writes=pk(0))
                kb.op("act", ACTF(ltz[0:ntz, :], ps[0][0:ntz, 0:512], AF.Exp, scale=-1.0), reads=pk(0), writes=[ltkz])
                kb.op("act", ACTF(ltz[0:ntz, :], ltz[0:ntz, :], AF.Ln, bias=1.0), reads=[ltkz], writes=[ltkz])

            for i in range(4):
                if i > 0:
                    wt, wk = W.take()
                for jj in range(2):
                    c = i * 2 + jj
                    if i == 0:
                        bnk = bq[jj]
                    else:
                        bnk = next_bank()
                        for k in range(KC):
                            kb.op("pe", MM(ps[bnk][:, 0:TB], wt[:, k, jj * 128:(jj + 1) * 128], xn[:, k, 0:TB], k == 0, k == KC - 1),
                                  reads=[wk, ("xn", k)], writes=pk(bnk), inc=(k == KC - 1))
                    if c < 4:
                        kb.op("act", ACTF(qk[:, c, 0:TB], ps[bnk][:, 0:TB], AF.Copy, scale=float(DK ** -0.5)),
                              reads=pk(bnk), writes=[("qk", c)])
                    else:
                        kb.op("dve", CP(qk[:, c, 0:TB], ps[bnk][:, 0:TB]), reads=pk(bnk), writes=[("qk", c)])
            emit_z(0)
            for i in range(4):
                wt, wk = W.take()
                for ti, (kind, c0, ntok, C) in enumerate(tiles):
                    bnk = next_bank()
                    for k in range(KC):
                        kb.op("pe", MM(ps[bnk][0:ntok, 0:256], xn[:, k, c0:c0 + ntok], wt[:, k, 0:256], k == 0, k == KC - 1),
                              reads=[wk, ("xn", k)], writes=pk(bnk), inc=(k == KC - 1))
                    if ti % 2 == 0:
                        kb.op("act", ACTF(vx[0:ntok, ti, i * 256:(i + 1) * 256], ps[bnk][0:ntok, 0:256], AF.Copy),
                              reads=pk(bnk), writes=[("vx", ti)])
                    else:
                        kb.op("dve", CP(vx[0:ntok, ti, i * 256:(i + 1) * 256], ps[bnk][0:ntok, 0:256]),
                              reads=pk(bnk), writes=[("vx", ti)])
            for i in range(4):
                wt, wk = W.take()
                for jj in range(2):
                    c = i * 2 + jj
                    bnk = next_bank()
                    for k in range(KC):
                        kb.op("pe", MM(ps[bnk][:, 0:TB], wt[:, k, jj * 128:(jj + 1) * 128], xn[:, k, 0:TB], k == 0, k == KC - 1),
                              reads=[wk, ("xn", k)], writes=pk(bnk), inc=(k == KC - 1))
                    tmp = tmpA if (c % 2 == 0) else tmpB
                    tk = ("tmp", c % 2)
                    kb.op("act", ACTF(tmp[:, 0:TB], ps[bnk][:, 0:TB], AF.Silu), reads=pk(bnk), writes=[tk])
                    kb.op("dve", TS(sgc[:, c, 0:TB], tmp[:, 0:TB], vcol(96, l, c), None, ALU.mult), reads=[tk, ("vecs", 0)], writes=[("sgc", c)])
            if has_s:
                for g4 in range(4):
                    cs_ = 2 + g4 % 2
                    st_ = stg(cs_)
                    kb.dma("sp", ch_cc2[g4 % 2], st_[0:120, :], cci[l, g4 * 4:(g4 + 1) * 4].rearrange("s t d -> (s t) d"), writes=stgk(cs_))
                    for half in range(2):
                        pb = ps[5 + half]
                        for kk_ in range(4):
                            k = half * 4 + kk_
                            kb.op("pe", TR(pb[:, kk_ * 128:kk_ * 128 + 120], st_[0:120, k * 128:(k + 1) * 128], ident[0:120, 0:120]),
                                  reads=stgk(cs_) + [("cm", 0)], writes=pk(5 + half), inc=(kk_ == 3))
                        src_v = pb[:, :].rearrange("p (a b) -> p a b", b=128)[:, :, 0:120].rearrange("p a (s t) -> p a s t", t=HIST)
                        kb.op("dve", CP(uext_s[:, half * 4:half * 4 + 4, g4 * 4:(g4 + 1) * 4, 0:HIST], src_v),
                              reads=pk(5 + half), writes=[("uext_s", k) for k in range(half * 4, half * 4 + 4)])
                kb.dma("sp", ch_dd, ccs[l, :, 0:HIST - DSEQ, :], cci[l, :, DSEQ:HIST, :])
            ufp_ = stg(2)
            ufp = ufp_.rearrange("p (k t) -> p k t", t=128)
            need_ufp = has_s or last
            for i in range(4):
                wa, wak = W.take()
                wg_, wgk = W.take()
                for jj in range(2):
                    c = i * 2 + jj
                    ba = next_bank(); bg = next_bank()
                    for k in range(KC):
                        kb.op("pe", MM(ps[ba][:, 0:TB], wa[:, k, jj * 128:(jj + 1) * 128], xn[:, k, 0:TB], k == 0, k == KC - 1),
                              reads=[wak, ("xn", k)], writes=pk(ba), inc=(k == KC - 1))
                    for k in range(KC):
                        kb.op("pe", MM(ps[bg][:, 0:TB], wg_[:, k, jj * 128:(jj + 1) * 128], xn[:, k, 0:TB], k == 0, k == KC - 1),
                              reads=[wgk, ("xn", k)], writes=pk(bg), inc=(k == KC - 1))
                    tmp = tmpA if (c % 2 == 0) else tmpB
                    tk = ("tmp", c % 2)
                    kb.op("act", ACTF(tmp[:, 0:TB], ps[bg][:, 0:TB], AF.Sigmoid), reads=pk(bg), writes=[tk])
                    kb.op("dve", TT(uext_p[:, c, HIST:HIST + NP], ps[ba][:, pc0:pc0 + NP], tmp[:, pc0:pc0 + NP], ALU.mult),
                          reads=pk(ba) + [tk], writes=[("uext_p", c)])
                    if has_s:
                        kb.op("dve", TT(uext_s[:, c, :, HIST:HIST + DSEQ], ps[ba][:, 0:128].rearrange("p (s t) -> p s t", t=DSEQ),
                                        tmp[:, 0:128].rearrange("p (s t) -> p s t", t=DSEQ), ALU.mult),
                              reads=pk(ba) + [tk], writes=[("uext_s", c)])
                        kb.op("dve", TT(ufp[:, c, 0:128], ps[ba][:, 0:128], tmp[:, 0:128], ALU.mult),
                              reads=pk(ba) + [tk], writes=stgk(2))
                    if last:
                        kb.op("dve", TT(ufp[:, c, 0:HIST], ps[ba][:, TB - HIST:TB], tmp[:, TB - HIST:TB], ALU.mult),
                              reads=pk(ba) + [tk], writes=stgk(2))
            kb.op("act", ACTF(hist[:, l, :, :], uext_p[:, :, NP:NP + HIST], AF.Copy),
                  reads=[("uext_p", k) for k in range(KC)], writes=[("hist", l)])
            if need_ufp:
                n_u = 128 if has_s else HIST
                st_ = stg(3)
                for half in range(2):
                    pb = ps[5 + half]
                    for kk_ in range(4):
                        k = half * 4 + kk_
                        kb.op("pe", TR(pb[0:n_u, kk_ * 128:(kk_ + 1) * 128], ufp[:, k, 0:n_u], ident),
                              reads=stgk(2) + [("cm", 0)], writes=pk(5 + half), inc=(kk_ == 3))
                    kb.op("dve", CP(st_[0:n_u, half * 512:(half + 1) * 512], pb[0:n_u, :]), reads=pk(5 + half), writes=stgk(3))
                if has_s:
                    for s_ in range(NSEQ):
                        kb.dma("sp", ch_cc, ccs[l, s_, HIST - DSEQ:HIST, :], st_[s_ * DSEQ:(s_ + 1) * DSEQ, :], reads=stgk(3), cont=(s_ > 0))
                else:
                    kb.dma("sp", ch_cc, ccp[l, :, :], st_[0:HIST, :], reads=stgk(3))

            if has_s:
                cpre = qk
                cpk = lambda c: [("qk", c)]
            else:
                cpre = A1[:, 0:KC * TBM].rearrange("p (k t) -> p k t", t=TBM)
                cpk = lambda c: a1keys(c * TBM * 4, TBM * 4)
            dseq = [(c_, j_) for c_ in range(KC) for j_ in range(CW)]
            dst_ = dict(built=0, used=0)
            cbank = {}

            def build_diag(i):
                c_, j_ = dseq[i]
                ds = i % NDIAG
                wcol = wdwT[:, l, j_ * 8 + c_:j_ * 8 + c_ + 1]
                if ds % 2 == 0:
                    kb.op("dve", TS(diag[:, ds, :], ident, wcol, None, ALU.mult), reads=[("cm", 0), ("wdwT", l)], writes=[("diag", ds)])
                else:
                    kb.op("act", ACTF(diag[:, ds, :], ident, AF.Copy, scale=wcol), reads=[("cm", 0), ("wdwT", l)], writes=[("diag", ds)])

            def prefetch_diags(ahead=NDIAG):
                while dst_["built"] < len(dseq) and dst_["built"] < dst_["used"] + ahead:
                    build_diag(dst_["built"]); dst_["built"] += 1

            def conv_part(c, j0, j1):
                if j0 == 0:
                    cbank[c] = (next_bank(), next_bank()) if has_s else (7, None)
                bnk, bnk2 = cbank[c]
                for j in range(j0, j1):
                    i = c * CW + j
                    assert i == dst_["used"]
                    if i >= dst_["built"]:
                        build_diag(i); dst_["built"] = i + 1
                    ds = i % NDIAG
                    dk_ = ("diag", ds)
                    kb.op("pe", MM(ps[bnk][:, pc0:pc0 + NP], diag[:, ds, :], uext_p[:, c, j:j + NP], j == 0, j == CW - 1),
                          reads=[dk_, ("uext_p", c)], writes=pk(bnk), inc=True)
                    if has_s:
                        kb.op("pe", MM(ps[bnk2][:, 0:128].rearrange("p (s t) -> p s t", t=DSEQ), diag[:, ds, :], uext_s[:, c, :, j:j + DSEQ], j == 0, j == CW - 1),
                              reads=[dk_, ("uext_s", c)], writes=pk(bnk2), inc=True)
                    dst_["used"] = i + 1
                if j1 == CW:
                    kb.op("dve", TS(cpre[:, c, pc0:pc0 + NP], ps[bnk][:, pc0:pc0 + NP], vcol(160, l, c), None, ALU.add),
                          reads=pk(bnk) + [("vecs", 0)], writes=cpk(c))
                    if has_s:
                        kb.op("dve", TS(cpre[:, c, 0:128], ps[bnk2][:, 0:128], vcol(160, l, c), None, ALU.add),
                              reads=pk(bnk2) + [("vecs", 0)], writes=cpk(c))

            def conv_chunk(c):
                conv_part(c, 0, CW)
                prefetch_diags()

            if not has_s:
                prefetch_diags()

            def H4(t):
                return t.rearrange("p (h t) -> p h t", t=128)

            pend_epi = []
            qkeys = [("qk", i) for i in range(4)]
            kkeys = [("qk", 4 + i) for i in range(4)]
            for ti, (kind, c0, ntok, C) in enumerate(tiles):
                nch = ntok // C
                lt = Ltok[:, ti % 2, :]
                ltk = ("Ltok", ti % 2)
                for hh in range(NH):
                    kb.op("pe", MM(ps[1][:, hh * 128:hh * 128 + ntok], lt[0:ntok, hh * 128:(hh + 1) * 128], Umat[C][0:ntok, 0:ntok], True, True),
                          reads=[ltk, ("cm", 0)], writes=pk(1), inc=(hh == NH - 1))
                for hh in range(NH):
                    kb.op("pe", MM(ps[2][:, hh * 128:hh * 128 + ntok], lt[0:ntok, hh * 128:(hh + 1) * 128], Rmat[C][0:ntok, 0:ntok], True, True),
                          reads=[ltk, ("cm", 0)], writes=pk(2), inc=(hh == NH - 1))
                if ti + 1 < len(tiles):
                    emit_z(ti + 1)
                if not has_s:
                    conv_part(2 * ti, 0, CW)
                b1 = H4(ps[1][:, :])[:, :, 0:ntok]
                b2 = H4(ps[2][:, :])[:, :, 0:ntok]
                kb.op("act", ACTF(expb[:, :, 0:ntok], b1, AF.Exp), reads=pk(1), writes=[("g_expb", 0)])
                kb.op("act", ACTF(expnb[:, :, 0:ntok], b1, AF.Exp, scale=-1.0), reads=pk(1), writes=[("g_expnb", 0)])
                kb.op("act", ACTF(expd[:, :, 0:ntok], b2, AF.Exp), reads=pk(2), writes=[("g_expd", 0)])
                qv = qk[:, 0:4, c0:c0 + ntok]
                kv = qk[:, 4:8, c0:c0 + ntok]
                kb.op("dve", TT(qt[:, :, 0:ntok], qv, expb[:, :, 0:ntok], ALU.mult), reads=qkeys + [("g_expb", 0)], writes=[("g_qt", 0)])
                kb.op("dve", TT(kt[:, :, 0:ntok], kv, expnb[:, :, 0:ntok], ALU.mult), reads=kkeys + [("g_expnb", 0)], writes=[("g_kt", 0)])
                kb.op("dve", TT(kkT[:, :, 0:ntok], kv, expd[:, :, 0:ntok], ALU.mult), reads=kkeys + [("g_expd", 0)], writes=[("g_kkT", 0)])
                qt32 = expnb
                if kind == "s":
                    kb.op("dve", TT(qt32[:, :, 0:ntok], qv, expb[:, :, 0:ntok], ALU.mult), reads=qkeys + [("g_expb", 0)], writes=[("g_expnb", 0)])
                if not has_s:
                    prefetch_diags()
                if pend_epi:
                    pend_epi.pop(0)()
                for hh in range(NH):
                    kb.op("pe", MM(ps[3][0:ntok, hh * 128:hh * 128 + ntok], kt[:, hh, 0:ntok], qt[:, hh, 0:ntok], True, True),
                          reads=[("g_kt", 0), ("g_qt", 0)], writes=pk(3), inc=(hh == NH - 1))
                for hh in range(NH):
                    kb.op("pe", TR(ps[4][0:ntok, hh * 128:(hh + 1) * 128], kkT[:, hh, 0:ntok], ident),
                          reads=[("g_kkT", 0), ("cm", 0)], writes=pk(4), inc=(hh == NH - 1))
                if not has_s:
                    conv_part(2 * ti + 1, 0, 15)
                for hh in range(NH):
                    kb.op("dve", TT(scm[0:ntok, hh, 0:ntok], ps[3][0:ntok, hh * 128:hh * 128 + ntok], Mmask[C][0:ntok, 0:ntok], ALU.mult),
                          reads=pk(3) + [("cm", 0)], writes=[("g_scm", 0)])
                kb.op("act", ACTF(kk[0:ntok, :, :], H4(ps[4][0:ntok, :]), AF.Copy), reads=pk(4), writes=[("g_kk", 0)])
                if not has_s:
                    prefetch_diags()
                def s_load(jn):
                    kb.dma("sp", ch_st[jn % 4], stg(jn % 4).rearrange("p (h v) -> p h v", v=DV),
                           sgi[l, jn].rearrange("h d v -> d h v"), writes=stgk(jn % 4))
                if kind == "s":
                    for jn in range(3):
                        s_load(jn)
                for j in range(nch):
                    sp_ = (par["p"] + j) % 2
                    if kind == "s":
                        slot = j % 4
                        sst = stg(slot).rearrange("p (h v) -> p h v", v=DV)
                        if j + 3 < nch:
                            s_load(j + 3)
                        km = j % 2
                        if j == 0:
                            kb.op("dve", TS(kkm[:, 0, :, :], kk[:, :, :], seqmask[:, 0:1], None, ALU.mult),
                                  reads=[("g_kk", 0), ("seqmask", 0)], writes=[("kkm", 0)])
                    def emit_o():
                        for hh in range(NH):
                            ob = 5 + hh // 2
                            for e_ in range(2):
                                oc = (hh % 2) * 256 + e_ * 128
                                if kind == "s":
                                    kb.op("pe", MM(ps[ob][:, oc + j * C:oc + (j + 1) * C], sst[:, hh, e_ * 128:(e_ + 1) * 128],
                                                   qt32[:, hh, j * C:(j + 1) * C], True, False),
                                          reads=stgk(slot) + [("g_expnb", 0)], writes=pk(ob), inc=False)
                                else:
                                    kb.op("pe", MM(ps[ob][:, oc + j * C:oc + (j + 1) * C], S_b[:, sp_, hh, e_ * 128:(e_ + 1) * 128],
                                                   qt[:, hh, j * C:(j + 1) * C], True, False),
                                          reads=[("S_b", sp_), ("g_qt", 0)], writes=pk(ob), inc=False)
                                kb.op("pe", MM(ps[ob][:, oc + j * C:oc + (j + 1) * C], vx[0:ntok, ti, hh * 256 + e_ * 128:hh * 256 + (e_ + 1) * 128],
                                               scm[0:ntok, hh, j * C:(j + 1) * C], False, True),
                                      reads=[("vx", ti), ("g_scm", 0)], writes=pk(ob), inc=True)
                    def emit_upd():
                        for hh in range(NH):
                            ub = 1 + hh // 2; uc = (hh % 2) * 256
                            if kind == "s":
                                kb.op("pe", MM(ps[ub][:, uc:uc + 256], kkm[:, km, hh, :], vx[0:128, ti, hh * 256:(hh + 1) * 256], True, True),
                                      reads=[("kkm", km), ("vx", ti)], writes=pk(ub), inc=(hh % 2 == 1))
                            else:
                                kb.op("pe", MM(ps[ub][:, uc:uc + 256], kk[j * C:(j + 1) * C, hh, :], vx[j * C:(j + 1) * C, ti, hh * 256:(hh + 1) * 256], True, True),
                                      reads=[("g_kk", 0), ("vx", ti)], writes=pk(ub), inc=(hh % 2 == 1))
                    if kind == "s":
                        emit_o(); emit_upd()
                    else:
                        emit_upd(); emit_o()
                    if not has_s and j == 0:
                        conv_part(2 * ti + 1, 15, CW)
                    if kind == "s" and j + 1 < nch:
                        kb.op("dve", TS(kkm[:, (j + 1) % 2, :, :], kk[:, :, :], seqmask[:, j + 1:j + 2], None, ALU.mult),
                              reads=[("g_kk", 0), ("seqmask", 0)], writes=[("kkm", (j + 1) % 2)])
                    for hh in range(NH):
                        ub = 1 + hh // 2; uc = (hh % 2) * 256
                        dec = expb[:, hh, (j + 1) * C - 1:(j + 1) * C]
                        if kind == "s":
                            kb.op("dve", STT(sst[:, hh, :], sst[:, hh, :], dec, ps[ub][:, uc:uc + 256], ALU.mult, ALU.add),
                                  reads=stgk(slot) + [("g_expb", 0)] + pk(ub), writes=stgk(slot))
                        else:
                            kb.op("dve", STT(S_f[:, l, hh, :], S_f[:, l, hh, :], dec, ps[ub][:, uc:uc + 256], ALU.mult, ALU.add),
                                  reads=[("S_f", l, hh), ("g_expb", 0)] + pk(ub), writes=[("S_f", l, hh)])
                    if kind == "s":
                        kb.dma("pool", ch_sto[slot], sgs[l, j].rearrange("h d v -> d h v"), sst, reads=stgk(slot))
                    else:
                        kb.op("act", ACTF(S_b[:, 1 - sp_, :, :], S_f[:, l, :, :], AF.Copy), reads=[("S_f", l, hh) for hh in range(NH)],
                              writes=[("S_b", 1 - sp_)])
                if kind != "s":
                    par["p"] = (par["p"] + nch) % 2
                for b_ in range(2):
                    kb.op("act", ACTF(osq[:, 4 * b_:4 * b_ + 4, 0:ntok], H4(ps[5 + b_][:, :])[:, :, 0:ntok], AF.Square),
                          reads=pk(5 + b_), writes=[("osq", b_)])
                if not has_s:
                    prefetch_diags()
                def epi2(ti=ti, c0=c0, ntok=ntok):
                    for hh in range(NH):
                        for e_ in range(2):
                            kb.op("pe", MM(ps[3][:, hh * 128:hh * 128 + ntok], ones_h[:], osq[:, hh * 2 + e_, 0:ntok], e_ == 0, e_ == 1),
                                  reads=[("osq", hh // 2), ("ones_h", 0)], writes=pk(3), inc=(hh == NH - 1 and e_ == 1))
                    rsqrt_ps(rstdh[:, :, 0:ntok], H4(ps[3][:, :])[:, :, 0:ntok], pk(3), [("rstdh", 0)])
                    for e_ in range(2):
                        kb.op("dve", TT(otmp[:, e_:8:2, 0:ntok], sgc[:, e_:8:2, c0:c0 + ntok], rstdh[:, :, 0:ntok], ALU.mult),
                              reads=[("sgc", c) for c in range(e_, 8, 2)] + [("rstdh", 0)], writes=[("otmp", e_)])
                    for b_ in range(2):
                        kb.op("dve", TT(ogla[:, 4 * b_:4 * b_ + 4, c0:c0 + ntok], H4(ps[5 + b_][:, :])[:, :, 0:ntok], otmp[:, 4 * b_:4 * b_ + 4, 0:ntok], ALU.mult),
                              reads=pk(5 + b_) + [("otmp", 0), ("otmp", 1)], writes=[("ogla", c) for c in range(4 * b_, 4 * b_ + 4)])

                pend_epi.append(epi2)
            while pend_epi:
                pend_epi.pop(0)()
            if has_s:
                for c in range(KC):
                    conv_chunk(c)
            rmsnorm_stats(cpre[:, :, 0:TB], [k_ for k in range(KC) for k_ in cpk(k)], TB)
            for c in range(KC):
                kb.op("dve", STT(cpre[:, c, 0:TB], cpre[:, c, 0:TB], vcol(128, l, c), rstd[:, 0:TB], ALU.mult, ALU.mult),
                      reads=cpk(c) + [("rstd", 0), ("vecs", 0)], writes=cpk(c))
                kb.op("act", ACTF(sgc[:, c, 0:TB], cpre[:, c, 0:TB], AF.Silu), reads=cpk(c), writes=[("sgc", c)])
            for i in range(4):
                wp, wpk = W.take()
                wa, wak = W.take()
                wb_, wbk = W.take()
                for jj in range(2):
                    c = i * 2 + jj
                    ba = next_bank(); bb = next_bank(); bo = next_bank()
                    for k in range(KC):
                        kb.op("pe", MM(ps[ba][:, 0:TB], wa[:, k, jj * 128:(jj + 1) * 128], xn[:, k, 0:TB], k == 0, k == KC - 1),
                              reads=[wak, ("xn", k)], writes=pk(ba), inc=(k == KC - 1))
                    for k in range(KC):
                        kb.op("pe", MM(ps[bb][:, 0:TB], wb_[:, k, jj * 128:(jj + 1) * 128], xn[:, k, 0:TB], k == 0, k == KC - 1),
                              reads=[wbk, ("xn", k)], writes=pk(bb), inc=(k == KC - 1))
                    for k in range(KC):
                        kb.op("pe", MM(ps[bo][:, 0:TB], wp[:, k, jj * 128:(jj + 1) * 128], sgc[:, k, 0:TB], k == 0, k == KC - 1),
                              reads=[wpk, ("sgc", k)], writes=pk(bo), inc=(k == KC - 1))
                    kb.op("act", ACTF(tmpA[:, 0:TB], ps[ba][:, 0:TB], AF.Sigmoid), reads=pk(ba), writes=[("tmp", 0)])
                    kb.op("act", ACTF(tmpB[:, 0:TB], ps[bb][:, 0:TB], AF.Sigmoid), reads=pk(bb), writes=[("tmp", 1)])
                    kb.op("dve", TT(tmpA[:, 0:TB], tmpA[:, 0:TB], ogla[:, c, 0:TB], ALU.mult), reads=[("tmp", 0), ("ogla", c)], writes=[("tmp", 0)])
                    kb.op("dve", TT(tmpB[:, 0:TB], tmpB[:, 0:TB], ps[bo][:, 0:TB], ALU.mult), reads=[("tmp", 1)] + pk(bo), writes=[("tmp", 1)])
                    kb.op("dve", TT(ogla[:, c, 0:TB], tmpA[:, 0:TB], tmpB[:, 0:TB], ALU.add), reads=[("tmp", 0), ("tmp", 1)], writes=[("ogla", c)])
            for i in range(4):
                wt, wk = W.take()
                for jj in range(2):
                    c = i * 2 + jj
                    bnk = next_bank()
                    for k in range(KC):
                        kb.op("pe", MM(ps[bnk][:, 0:TB], wt[:, k, jj * 128:(jj + 1) * 128], ogla[:, k, 0:TB], k == 0, k == KC - 1),
                              reads=[wk, ("ogla", k)], writes=pk(bnk), inc=(k == KC - 1))
                    kb.op("dve", TT(h[:, c, 0:TB], h[:, c, 0:TB], ps[bnk][:, 0:TB], ALU.add), reads=[hk(c)] + pk(bnk), writes=[hk(c)])
                    esq_chunk(c, TB, depth=3)
            if last:
                kb.dma("sp", ch_misc, sgp[l].rearrange("h d v -> d h v"), S_f[:, l, :, :], reads=[("S_f", l, hh) for hh in range(NH)])

        for bi, blk in enumerate(blocks):
            TB = blk["TB"]
            load_block_input(bi, blk)
            if stop == "load":
                break
            for l in range(nl):
                ffn(l, 0, TB, l > 0)
                if stop == "ffn1":
                    break
                mixer(l, bi, blk)
                if stop == "mixer":
                    break
                ffn(l, 64, TB, True)
            if stop is not None:
                break
            store_block_output(bi, blk)
        while pend_store:
            emit_store(pend_store.pop(0))
        assert stop is not None or W.i == len(slabs), (W.i, len(slabs))
        kb.wait_all("sp")
        kb.run(block)
    return nc


_NC_CACHE = {}


def kernel(**inputs):
    f32 = lambda a: np.ascontiguousarray(np.asarray(a, dtype=np.float32))
    x_prompt = f32(inputs["x_prompt"]); x_sample = f32(inputs["x_sample"])
    state_gla = f32(inputs["state_gla"]); cache_conv = f32(inputs["cache_conv"])
    consts = make_consts()
    cmat = np.stack([consts[n] for n in CONST_ORDER]).astype(np.float32)
    shared = {
        "meta": f32(inputs["meta_tokens"]),
        "norm_ffn1": f32(inputs["norm_ffn1"]), "wg1": f32(inputs["w_ffn1_gate"]), "wu1": f32(inputs["w_ffn1_up"]), "wd1": f32(inputs["w_ffn1_down"]),
        "norm_mix": f32(inputs["norm_mix"]), "w_in": f32(inputs["w_in"]), "w_dec": f32(inputs["w_decay_up"]), "b_dec": f32(inputs["b_decay"]),
        "gla_norm": f32(inputs["gla_norm"]), "w_dw": f32(inputs["w_dw"]), "b_dw": f32(inputs["b_dw"]), "conv_norm": f32(inputs["conv_norm"]),
        "w_pw": f32(inputs["w_pw"]), "w_out": f32(inputs["w_out"]),
        "norm_ffn2": f32(inputs["norm_ffn2"]), "wg2": f32(inputs["w_ffn2_gate"]), "wu2": f32(inputs["w_ffn2_up"]), "wd2": f32(inputs["w_ffn2_down"]),
        "norm_final": f32(inputs["norm_final"]).reshape(1, D),
        "cmat": cmat, "seqmask": consts["seqmask"],
    }
    n = 8
    in_maps = []
    for c in range(n):
        m = dict(shared)
        m["xp"] = x_prompt[c]
        m["xs"] = x_sample[c * NSEQ:(c + 1) * NSEQ].reshape(NSEQ * DSEQ, D)
        m["sgi"] = np.ascontiguousarray(state_gla[:, c * NSEQ:(c + 1) * NSEQ])
        m["cci"] = np.ascontiguousarray(cache_conv[:, c * NSEQ:(c + 1) * NSEQ])
        in_maps.append(m)
    if "nc" not in _NC_CACHE:
        _NC_CACHE["nc"] = build_program()
    nc = _NC_CACHE["nc"]
    res = run_bass_kernel_spmd(nc, in_maps, core_ids=list(range(n)))
    r = res.results
    y_prompt = np.stack([r[c]["yp"] for c in range(n)], axis=0)
    y_sample = np.concatenate([r[c]["ys"].reshape(NSEQ, DSEQ, D) for c in range(n)], axis=0)
    sg_p = np.stack([r[c]["sgp"] for c in range(n)], axis=1)
    cc_p = np.stack([r[c]["ccp"] for c in range(n)], axis=1)
    sg_s = np.concatenate([r[c]["sgs"] for c in range(n)], axis=1)
    cc_s = np.concatenate([r[c]["ccs"] for c in range(n)], axis=1)
    return (y_prompt.astype(np.float32), y_sample.astype(np.float32), sg_p.astype(np.float32), cc_p.astype(np.float32),
            sg_s.astype(np.float32), cc_s.astype(np.float32))
```
